# Optimizing a Trainium2 kernel written in Bass

```python
import math
import jax
import jax.numpy as jnp
from jax import lax
import numpy as np

D_MODEL = 1024
BATCH = 8
SEQ = 4096
DEPTH = 2
DEC_BATCH = 2
DEC_SEQ = 16384
PAST_LEN = 128

MIX_WIDTH = D_MODEL
ATT_WIDTH = MIX_WIDTH // 2
MLSTM_WIDTH = MIX_WIDTH - ATT_WIDTH
ATT_HEADS = 4
ATT_VDIM = ATT_WIDTH // ATT_HEADS
ATT_QKDIM = ATT_VDIM // 2
MLSTM_HEADS = 4
MLSTM_DH = MLSTM_WIDTH // MLSTM_HEADS
N_GATES = 4 * MLSTM_HEADS
CONV_W = 5
FFN_HIDDEN = ((8 * D_MODEL + 3 * 256 - 1) // (3 * 256)) * 256
Q_BLOCK = 128
CHUNK = 128
EPS = 1e-6
ATT_QK_COLS = ATT_HEADS * 2 * ATT_QKDIM
IN_COLS = 2 * ATT_QK_COLS + ATT_WIDTH + 4 * MLSTM_WIDTH + N_GATES
SPLIT_POINTS = (
    ATT_QK_COLS,
    2 * ATT_QK_COLS,
    2 * ATT_QK_COLS + ATT_WIDTH,
    2 * ATT_QK_COLS + ATT_WIDTH + 2 * MLSTM_WIDTH,
    2 * ATT_QK_COLS + ATT_WIDTH + 3 * MLSTM_WIDTH,
    2 * ATT_QK_COLS + ATT_WIDTH + 4 * MLSTM_WIDTH,
)

kernel_name = "hybrid_diffattn_mlstm_encoder"


def rms_norm(x, w):
    xf = x.astype(jnp.float32)
    y = xf * lax.rsqrt(jnp.mean(xf * xf, axis=-1, keepdims=True) + EPS)
    return (y * w.astype(jnp.float32)).astype(x.dtype)


def centred_conv(x, w, b):
    c = x.shape[-1]
    y = lax.conv_general_dilated(
        x, w[:, None, :].astype(x.dtype), window_strides=(1,),
        padding=[(CONV_W // 2, CONV_W // 2)],
        dimension_numbers=('NWC', 'WIO', 'NWC'), feature_group_count=c)
    return y + b.astype(x.dtype)


def diff_attention(q, k, v, lam, subln_w, layer_idx):
    bsz, seq = q.shape[0], q.shape[1]
    n_blocks = seq // Q_BLOCK
    lam_init = 0.8 - 0.6 * math.exp(-0.3 * layer_idx)
    lf = lam.astype(jnp.float32)
    lam_full = jnp.exp(jnp.sum(lf[0] * lf[1])) - jnp.exp(jnp.sum(lf[2] * lf[3])) + lam_init
    slopes = 2.0 ** (-8.0 * jnp.arange(1, ATT_HEADS + 1, dtype=jnp.float32) / ATT_HEADS)
    k_t = jnp.transpose(k, (0, 2, 3, 1, 4))
    v_t = jnp.transpose(v, (0, 2, 1, 3)).astype(jnp.float32)
    q_b = q.reshape(bsz, n_blocks, Q_BLOCK, ATT_HEADS, 2, ATT_QKDIM).transpose(1, 0, 3, 4, 2, 5)
    k_pos = jnp.arange(seq, dtype=jnp.float32)
    scale = ATT_QKDIM ** -0.5

    def one_block(args):
        q_i, blk = args
        s = jnp.einsum('bhmqd,bhmkd->bhmqk', q_i, k_t,
                       preferred_element_type=jnp.float32) * scale
        q_pos = (blk * Q_BLOCK + jnp.arange(Q_BLOCK)).astype(jnp.float32)
        alibi = -slopes[:, None, None] * jnp.abs(q_pos[:, None] - k_pos[None, :])
        p = jax.nn.softmax(s + alibi[None, :, None], axis=-1)
        a = p[:, :, 0] - lam_full * p[:, :, 1]
        return jnp.einsum('bhqk,bhkd->bhqd', a, v_t)

    o = lax.map(one_block, (q_b, jnp.arange(n_blocks)))
    o = o.transpose(1, 0, 3, 2, 4).reshape(bsz, seq, ATT_HEADS, ATT_VDIM)
    o = rms_norm(o, subln_w) * (1.0 - lam_init)
    return o.reshape(bsz, seq, ATT_WIDTH).astype(v.dtype)


def mlstm_direction(q, k, v, ig, fg):
    bsz, nh, seq, dh = q.shape
    nc = seq // CHUNK
    qc = q.reshape(bsz, nh, nc, CHUNK, dh)
    kc = k.reshape(bsz, nh, nc, CHUNK, dh)
    vc = v.reshape(bsz, nh, nc, CHUNK, dh)
    ic = ig.reshape(bsz, nh, nc, CHUNK)
    g = jnp.cumsum(jax.nn.log_sigmoid(fg).reshape(bsz, nh, nc, CHUNK), axis=-1)
    g_end = g[..., -1]
    w = g_end[..., None] - g + ic
    m_loc = jnp.max(w, axis=-1)
    e = jnp.exp(w - m_loc[..., None])
    c_loc = jnp.einsum('bhcld,bhcle->bhcde', kc * e[..., None], vc)
    n_loc = jnp.einsum('bhcl,bhcld->bhcd', e, kc)

    def step(carry, xs):
        c_st, n_st, m_st = carry
        ge, ml, cl, nl = xs
        m_new = jnp.maximum(ge + m_st, ml)
        a = jnp.exp(ge + m_st - m_new)
        b = jnp.exp(ml - m_new)
        c_new = a[..., None, None] * c_st + b[..., None, None] * cl
        n_new = a[..., None] * n_st + b[..., None] * nl
        return (c_new, n_new, m_new), (c_st, n_st, m_st)

    init = (jnp.zeros((bsz, nh, dh, dh), jnp.float32),
            jnp.zeros((bsz, nh, dh), jnp.float32),
            jnp.zeros((bsz, nh), jnp.float32))
    xs = (jnp.moveaxis(g_end, 2, 0), jnp.moveaxis(m_loc, 2, 0),
          jnp.moveaxis(c_loc, 2, 0), jnp.moveaxis(n_loc, 2, 0))
    _, (c_prev, n_prev, m_prev) = lax.scan(step, init, xs)
    c_prev = jnp.moveaxis(c_prev, 0, 2)
    n_prev = jnp.moveaxis(n_prev, 0, 2)
    m_prev = jnp.moveaxis(m_prev, 0, 2)

    a_log = g + m_prev[..., None]
    causal = jnp.tril(jnp.ones((CHUNK, CHUNK), dtype=bool))
    d_log = jnp.where(causal, g[..., :, None] - g[..., None, :] + ic[..., None, :], -jnp.inf)
    m_out = jnp.maximum(a_log, jnp.max(d_log, axis=-1))
    s = jnp.einsum('bhcjd,bhcsd->bhcjs', qc, kc) * jnp.exp(d_log - m_out[..., None])
    inter = jnp.exp(a_log - m_out)
    num = (inter[..., None] * jnp.einsum('bhcjd,bhcde->bhcje', qc, c_prev)
           + jnp.einsum('bhcjs,bhcse->bhcje', s, vc))
    den = inter * jnp.einsum('bhcjd,bhcd->bhcj', qc, n_prev) + jnp.sum(s, axis=-1)
    h = num / jnp.maximum(jnp.abs(den), jnp.exp(-m_out))[..., None]
    return h.reshape(bsz, nh, seq, dh)


def mlstm_mixer(q, k, v, o, gates, norm_w):
    bsz, seq = q.shape[0], q.shape[1]

    def heads(t):
        return t.reshape(bsz, seq, MLSTM_HEADS, MLSTM_DH).transpose(0, 2, 1, 3).astype(jnp.float32)

    qh, kh, vh = heads(q), heads(k) * (MLSTM_DH ** -0.5), heads(v)
    g = gates.astype(jnp.float32).transpose(0, 2, 1)
    i_f, i_b, f_f, f_b = jnp.split(g, 4, axis=1)
    h_fwd = mlstm_direction(qh, kh, vh, i_f, f_f)
    flip = lambda t: jnp.flip(t, axis=2)
    h_bwd = flip(mlstm_direction(flip(qh), flip(kh), flip(vh), flip(i_b), flip(f_b)))
    h = (h_fwd + h_bwd).transpose(0, 2, 1, 3)
    h = rms_norm(h, norm_w)
    out = jax.nn.sigmoid(o.astype(jnp.float32)).reshape(bsz, seq, MLSTM_HEADS, MLSTM_DH) * h
    return out.reshape(bsz, seq, MLSTM_WIDTH).astype(q.dtype)


def encoder_layer(x, layer_idx, norm1_w, w_in, b_gate, conv_w, conv_b, lam, att_norm_w,
                  mlstm_norm_w, w_out, norm2_w, w_ffn_in, w_ffn_out):
    bsz, seq = x.shape[0], x.shape[1]
    h = rms_norm(x, norm1_w)
    proj = h @ w_in
    aq, ak, av, mqk, mv, mo, gates = jnp.split(proj, list(SPLIT_POINTS), axis=-1)
    mqk = jax.nn.silu(centred_conv(mqk, conv_w, conv_b))
    mq, mk = jnp.split(mqk, 2, axis=-1)
    att = diff_attention(aq.reshape(bsz, seq, ATT_HEADS, 2, ATT_QKDIM),
                         ak.reshape(bsz, seq, ATT_HEADS, 2, ATT_QKDIM),
                         av.reshape(bsz, seq, ATT_HEADS, ATT_VDIM),
                         lam, att_norm_w, layer_idx)
    mem = mlstm_mixer(mq, mk, mv, mo, gates + b_gate.astype(gates.dtype), mlstm_norm_w)
    x = x + jnp.concatenate([att.astype(x.dtype), mem.astype(x.dtype)], axis=-1) @ w_out
    h = rms_norm(x, norm2_w)
    gate, up = jnp.split(h @ w_ffn_in, 2, axis=-1)
    return x + (jax.nn.silu(gate) * up) @ w_ffn_out


def run_trunk(x, norm1_w, w_in, b_gate, conv_w, conv_b, lam, att_norm_w, mlstm_norm_w,
              w_out, norm2_w, w_ffn_in, w_ffn_out, final_norm_w):
    for l in range(DEPTH):
        x = encoder_layer(x, l, norm1_w[l], w_in[l], b_gate[l], conv_w[l], conv_b[l], lam[l],
                          att_norm_w[l], mlstm_norm_w[l], w_out[l], norm2_w[l],
                          w_ffn_in[l], w_ffn_out[l])
    return rms_norm(x, final_norm_w)


def setup_inputs(seed: int = 0) -> dict:
    key = jax.random.key(seed)
    ks = jax.random.split(key, 16)
    f32 = jnp.float32

    def nrm(k, shape, s):
        return s * jax.random.normal(k, shape, f32)

    x_prompt = jax.random.normal(ks[0], (BATCH, SEQ, D_MODEL), f32)
    x_sample = jax.random.normal(ks[1], (DEC_BATCH, DEC_SEQ, D_MODEL), f32)
    norm1_w = 1.0 + nrm(ks[2], (DEPTH, D_MODEL), 0.01)
    w_in = nrm(ks[3], (DEPTH, D_MODEL, IN_COLS), D_MODEL ** -0.5)
    f_base = jnp.linspace(3.0, 6.0, MLSTM_HEADS, dtype=f32)
    b_gate = jnp.concatenate(
        [nrm(ks[4], (DEPTH, 2 * MLSTM_HEADS), 0.1),
         jnp.concatenate([f_base, f_base])[None, :] + nrm(ks[5], (DEPTH, 2 * MLSTM_HEADS), 0.1)],
        axis=-1)
    conv_w = nrm(ks[6], (DEPTH, CONV_W, 2 * MLSTM_WIDTH), CONV_W ** -0.5)
    conv_b = nrm(ks[7], (DEPTH, 2 * MLSTM_WIDTH), 0.01)
    lam = nrm(ks[8], (DEPTH, 4, ATT_QKDIM), 0.1)
    att_norm_w = 1.0 + nrm(ks[9], (DEPTH, ATT_VDIM), 0.01)
    mlstm_norm_w = 1.0 + nrm(ks[10], (DEPTH, MLSTM_HEADS, MLSTM_DH), 0.01)
    w_out = nrm(ks[11], (DEPTH, MIX_WIDTH, D_MODEL), MIX_WIDTH ** -0.5)
    norm2_w = 1.0 + nrm(ks[12], (DEPTH, D_MODEL), 0.01)
    w_ffn_in = nrm(ks[13], (DEPTH, D_MODEL, 2 * FFN_HIDDEN), D_MODEL ** -0.5)
    w_ffn_out = nrm(ks[14], (DEPTH, FFN_HIDDEN, D_MODEL), FFN_HIDDEN ** -0.5)
    final_norm_w = 1.0 + nrm(ks[15], (D_MODEL,), 0.01)
    return {"x_prompt": x_prompt, "x_sample": x_sample, "norm1_w": norm1_w, "w_in": w_in,
            "b_gate": b_gate, "conv_w": conv_w, "conv_b": conv_b, "lam": lam,
            "att_norm_w": att_norm_w, "mlstm_norm_w": mlstm_norm_w, "w_out": w_out,
            "norm2_w": norm2_w, "w_ffn_in": w_ffn_in, "w_ffn_out": w_ffn_out,
            "final_norm_w": final_norm_w}


def reference(x_prompt, x_sample, norm1_w, w_in, b_gate, conv_w, conv_b, lam, att_norm_w,
              mlstm_norm_w, w_out, norm2_w, w_ffn_in, w_ffn_out, final_norm_w):
    y_prompt = run_trunk(x_prompt, norm1_w, w_in, b_gate, conv_w, conv_b, lam, att_norm_w,
                         mlstm_norm_w, w_out, norm2_w, w_ffn_in, w_ffn_out, final_norm_w)
    y_sample = run_trunk(x_sample, norm1_w, w_in, b_gate, conv_w, conv_b, lam, att_norm_w,
                         mlstm_norm_w, w_out, norm2_w, w_ffn_in, w_ffn_out, final_norm_w)
    return (y_prompt, y_sample)
```

```python
import math
from contextlib import ExitStack

import numpy as np
import ml_dtypes
import concourse.bass as bass
import concourse.mybir as mybir
from concourse.bass_utils import run_bass_kernel_spmd

F32 = mybir.dt.float32
BF16 = mybir.dt.bfloat16
AF = mybir.ActivationFunctionType
ALU = mybir.AluOpType
AX = mybir.AxisListType

D = 1024
FF = 2816
EPS = 1e-6
NSLOT = 8
SAME_SYNC = True
ACOLS = 52800
PH = 4864
NBF = ml_dtypes.bfloat16
SLOPES = [2.0 ** (-2.0 * (h + 1)) for h in range(4)]


class Prog:
    def __init__(self, nc, stack):
        self.nc = nc
        self.engs = ['pe', 'act', 'dve', 'pool', 'sp']
        self.ops = {e: [] for e in self.engs}
        self.lastw = {}
        self.rd_e = {}
        self.rd_d = {}
        self.ndma = {e: 0 for e in self.engs}
        self.esem = {e: stack.enter_context(nc.semaphore("s_" + e)) for e in self.engs}
        self.dsem = {e: [stack.enter_context(nc.semaphore("d_%s%d" % (e, i))) for i in range(NSLOT)]
                     for e in ('sp', 'pool')}
        self.pending = {e: set() for e in self.engs}

    def add(self, eng, fn, r=(), w=(), dma=False):
        deps = set(self.pending[eng])
        self.pending[eng] = set()
        for b in r:
            t = self.lastw.get(b)
            if t is not None:
                deps.add(t)
        for b in w:
            t = self.lastw.get(b)
            if t is not None:
                deps.add(t)
            for e2, i2 in self.rd_e.get(b, {}).items():
                deps.add(('e', e2, i2))
            for t2 in self.rd_d.get(b, ()):
                deps.add(t2)
        idx = len(self.ops[eng])
        if dma:
            n = self.ndma[eng]
            self.ndma[eng] += 1
            tok = ('d', eng, n)
            if n >= NSLOT:
                deps.add(('d', eng, n - NSLOT))
        else:
            tok = ('e', eng, idx)
        deps = {t for t in deps
                if not (t[0] == 'e' and t[1] == eng and not dma and (eng == 'pe' or not SAME_SYNC))}
        for b in r:
            if dma:
                self.rd_d.setdefault(b, []).append(tok)
            else:
                self.rd_e.setdefault(b, {})[eng] = idx
        for b in w:
            self.lastw[b] = tok
            self.rd_e[b] = {}
            self.rd_d[b] = []
        self.ops[eng].append([fn, deps, tok, False])
        return tok

    def barrier(self):
        toks = set()
        for e in self.engs:
            for op in reversed(self.ops[e]):
                if op[2][0] == 'e':
                    toks.add(op[2])
                    break
            n = self.ndma[e]
            for k in range(max(0, n - NSLOT), n):
                toks.add(('d', e, k))
        for e in self.engs:
            self.pending[e] |= toks
        self.lastw = {}
        self.rd_e = {}
        self.rd_d = {}

    def emit(self, block):
        for e in self.engs:
            for op in self.ops[e]:
                for t in op[1]:
                    if t[0] == 'e':
                        self.ops[t[1]][t[2]][3] = True
        val = {}
        last = {e: 0 for e in self.engs}
        for e in self.engs:
            c = 0
            for i, op in enumerate(self.ops[e]):
                if op[3]:
                    c += 1
                    val[(e, i)] = c
            last[e] = c

        def semval(t):
            if t[0] == 'e':
                return ('e', t[1]), self.esem[t[1]], val[(t[1], t[2])]
            return (('d', t[1], t[2] % NSLOT), self.dsem[t[1]][t[2] % NSLOT],
                    16 * (t[2] // NSLOT + 1))

        def run(e, eo):
            waited = {}
            for fn, deps, tok, ms in self.ops[e]:
                need = {}
                for t in deps:
                    k, s, v = semval(t)
                    if waited.get(k, 0) < v and need.get(k, (None, 0))[1] < v:
                        need[k] = (s, v)
                for k, (s, v) in need.items():
                    eo.wait_ge(s, v)
                    waited[k] = v
                ins = fn(eo)
                if tok[0] == 'd':
                    ins.then_inc(self.dsem[e][tok[2] % NSLOT], 16)
                elif ms:
                    ins.then_inc(self.esem[e], 1)
            if e == 'sp':
                for q in ('sp', 'pool'):
                    n = self.ndma[q]
                    for k in range(max(0, n - NSLOT), n):
                        kk, s, v = semval(('d', q, k))
                        if waited.get(kk, 0) < v:
                            eo.wait_ge(s, v)
                            waited[kk] = v
                for q in self.engs:
                    if q != 'sp' and last[q] > 0:
                        eo.wait_ge(self.esem[q], last[q])

        @block.tensor
        def _(eo):
            run('pe', eo)

        @block.scalar
        def _(eo):
            run('act', eo)

        @block.vector
        def _(eo):
            run('dve', eo)

        @block.gpsimd
        def _(eo):
            run('pool', eo)

        @block.sync
        def _(eo):
            run('sp', eo)


def build(seq_lens, dbg=False, att_full=False, nlayers=2):
    nc = bass.Bass("TRN2", target_bir_lowering=False)
    Smax = max(seq_lens)

    def din(name, shape, dt=F32):
        return nc.dram_tensor(name, list(shape), dt, kind="ExternalInput").ap()

    def dscr(name, shape, dt):
        return nc.dram_tensor(name, list(shape), dt,
                              kind=("ExternalOutput" if dbg else "Internal")).ap()

    xs = [din("x%d" % i, [S, D]) for i, S in enumerate(seq_lens)]
    ys = [nc.dram_tensor("y%d" % i, [S, D], F32, kind="ExternalOutput").ap()
          for i, S in enumerate(seq_lens)]
    w_in = din("w_in", [2, D, 3600])
    w_out = din("w_out", [2, D, D])
    w_f1 = din("w_ffn_in", [2, D, 2 * FF])
    w_f2 = din("w_ffn_out", [2, FF, D])
    d_nw1 = din("norm1_w", [128, 16])
    d_nw2 = din("norm2_w", [128, 16])
    d_fnw = din("final_norm_w", [1, D])
    d_bg = din("b_gate", [1, 32])
    d_cw = din("conv_w", [128, 80])
    d_cb = din("conv_b", [128, 16])
    d_lam = din("lam", [1, 512])
    d_anw = din("att_norm_w", [1, 256])
    d_mnw = din("mlstm_norm_w", [1, 1024])
    d_cf = din("c_f32", [128, 1536])
    d_cbf = din("c_bf16", [128, 640], BF16)
    d_posq = din("posq", [2, 4, Smax], BF16)
    d_kpos = din("kpos", [4, 4, Smax], BF16)

    qT = dscr("s_qT", [4, 128, Smax], BF16)
    kT = dscr("s_kT", [4, 128, Smax], BF16)
    Vx = dscr("s_Vx", [Smax, 4, 129], BF16)
    MVx = dscr("s_MVx", [Smax, 4, 129], BF16)
    MO = dscr("s_MO", [Smax, 512], F32)
    G = dscr("s_G", [Smax, 16], F32)
    PRE = dscr("s_pre", [8, 128, Smax + 4], F32)
    mqT = dscr("s_mqT", [512, Smax], BF16)
    mkT = dscr("s_mkT", [512, Smax], BF16)
    HF = dscr("s_HF", [Smax, 512], F32)
    mixT = dscr("s_mixT", [1024, Smax], BF16)
    X1 = dscr("s_X1", [Smax, D], F32)

    with ExitStack() as st:
        P = Prog(nc, st)
        arena = st.enter_context(nc.sbuf_tensor("arena", [128, ACOLS], F32))
        psq = st.enter_context(nc.psum_tensor("psq", [128, 4, 512], F32))
        ps = [psq[:, i, :] for i in range(4)] + \
             [st.enter_context(nc.psum_tensor("ps%d" % i, [128, 512], F32))[:, :] for i in range(4, 8)]
        block = st.enter_context(nc.Block())

        def V(off, n, dt=F32, pat=None, **kw):
            ap = arena[:, off:off + n]
            if dt is BF16:
                ap = ap.bitcast(BF16)
            if pat:
                ap = ap.rearrange(pat, **kw)
            return ap

        def psb(i):
            return ps[i][:, :].bitcast(BF16)

        def dma(q, out, in_, r=(), w=()):
            P.add(q, lambda e: e.dma_start(out=out, in_=in_), r, w, dma=True)

        def mm(out, lhsT, rhs, start, stop, r=(), w=()):
            P.add('pe', lambda e: e.matmul(out, lhsT=lhsT, rhs=rhs, start=start, stop=stop), r, w)

        def tr(out, in_, ident, r=(), w=()):
            P.add('pe', lambda e: e.transpose(out, in_, ident), r, w)

        def act(out, in_, func, r=(), w=(), **kw):
            P.add('act', lambda e: e.activation(out=out, in_=in_, func=func, **kw), r, w)

        def cp(eng, out, in_, r=(), w=()):
            if eng == 'act':
                P.add('act', lambda e: e.copy(out=out, in_=in_), r, w)
            else:
                P.add(eng, lambda e: e.tensor_copy(out=out, in_=in_), r, w)

        def ts(eng, out, in0, s1, s2, op0, op1=None, r=(), w=()):
            if op1 is None:
                P.add(eng, lambda e: e.tensor_scalar(out=out, in0=in0, scalar1=s1, scalar2=None, op0=op0), r, w)
            else:
                P.add(eng, lambda e: e.tensor_scalar(out=out, in0=in0, scalar1=s1, scalar2=s2, op0=op0, op1=op1), r, w)

        def tt(eng, out, in0, in1, op, r=(), w=()):
            P.add(eng, lambda e: e.tensor_tensor(out=out, in0=in0, in1=in1, op=op), r, w)

        def stt(eng, out, in0, scalar, in1, op0, op1, r=(), w=()):
            P.add(eng, lambda e: e.scalar_tensor_tensor(out=out, in0=in0, scalar=scalar, in1=in1, op0=op0, op1=op1), r, w)

        def rsqrt_chain(v, n, key):
            act(v, v, AF.Sqrt, r=[key], w=[key])
            P.add('dve', lambda e: e.reciprocal(out=v, in_=v), [key], [key])

        o = 0
        cf = V(o, 1536); o += 1536
        cbf = V(o, 320, BF16); o += 320
        nw1 = V(o, 16); o += 16
        nw2 = V(o, 16); o += 16
        cwt = V(o, 80); o += 80
        cbt = V(o, 16); o += 16
        fnwb = V(o, 1024); o += 1024
        bgb = V(o, 32); o += 32
        lamb = V(o, 512); o += 512
        anwb = V(o, 256); o += 256
        mnwb = V(o, 1024); o += 1024
        nlam = V(o, 8); o += 8
        zer = V(o, 8); o += 8
        assert o <= PH
        ident_f = cf[:, 0:128]
        ones_f = cf[:, 128:256]
        tri = [cf[:, 256:384], cf[:, 384:512]]
        msk = [cf[:, 512:1024], cf[:, 1024:1536]]
        ident_b = cbf[:, 0:128]
        dcorr = [cbf[:, 128 + 128 * h:256 + 128 * h] for h in range(4)]

        dma('sp', cf, d_cf[:, :])
        dma('sp', cbf, d_cbf[:, :])
        dma('sp', nw1, d_nw1[:, :])
        dma('sp', nw2, d_nw2[:, :])
        dma('sp', cwt, d_cw[:, :])
        dma('sp', cbt, d_cb[:, :])
        dma('sp', fnwb, d_fnw.partition_broadcast(128))
        dma('sp', bgb, d_bg.partition_broadcast(128))
        dma('sp', lamb, d_lam.partition_broadcast(128))
        dma('sp', anwb, d_anw.partition_broadcast(128))
        dma('sp', mnwb, d_mnw.partition_broadcast(128))
        P.add('dve', lambda e: e.memset(zer, 0.0), [], ['zer'])
        P.barrier()
        lam_init = [0.8 - 0.6 * math.exp(-0.3 * l) for l in range(2)]
        for l in range(2):
            lv = lamb[:, l * 256:(l + 1) * 256].rearrange("p (a b d) -> p a b d", a=2, b=2)
            pr = V(PH, 128, F32, "p (a d) -> p a d", a=2)
            sm2 = V(PH + 128, 2)
            tt('dve', pr, lv[:, :, 0, :], lv[:, :, 1, :], ALU.mult, r=[], w=['pr'])
            P.add('dve', lambda e, sm2=sm2, pr=pr: e.tensor_reduce(out=sm2, in_=pr, axis=AX.X, op=ALU.add), ['pr'], ['sm2'])
            act(sm2, sm2, AF.Exp, r=['sm2'], w=['sm2'])
            stt('dve', nlam[:, l:l + 1], sm2[:, 1:2], -lam_init[l], sm2[:, 0:1], ALU.add, ALU.subtract,
                r=['sm2'], w=['nlam'])
            ts('dve', anwb[:, l * 128:(l + 1) * 128], anwb[:, l * 128:(l + 1) * 128], 1.0 - lam_init[l], None,
               ALU.mult, r=[], w=['anwb'])
        P.barrier()

        def phase1(l, S, xin):
            NST = S // 512
            b = PH
            Wb = V(b, 14400, BF16, "p (c n) -> p c n", c=8); b += 14400
            stg = [V(b + i * 1800, 1800, F32, "p (c n) -> p c n", c=8) for i in range(2)]; b += 3600
            xt = [V(b + i * 4096, 4096, F32, "p (s d) -> p s d", s=4) for i in range(2)]; b += 8192
            xn = [V(b + i * 512, 512, BF16) for i in range(2)]; b += 1024
            hT = [V(b + i * 2048, 2048, BF16, "p (c t) -> p c t", c=8) for i in range(2)]; b += 4096
            junk = V(b, 512, BF16); b += 512
            smv = V(b, 16); b += 16
            fo = [V(b + i * 256, 256, BF16) for i in range(4)]; b += 1024
            pret = [V(b + i * 512, 512) for i in range(3)]; b += 1536
            vt = [V(b + i * 258, 258, BF16, "p (h e) -> p h e", h=4) for i in range(2)]; b += 516
            mvt = [V(b + i * 258, 258, BF16, "p (h e) -> p h e", h=4) for i in range(2)]; b += 516
            mot = [V(b + i * 512, 512) for i in range(2)]; b += 1024
            gtt = [V(b + i * 16, 16) for i in range(2)]; b += 32
            assert b <= ACOLS
            wv = w_in[l].rearrange("(c p) n -> p c n", p=128)
            for sl in range(16):
                sk = 'stg%d' % (sl % 2)
                dma('sp', stg[sl % 2], wv[:, :, sl * 225:(sl + 1) * 225], w=[sk])
                for ch in range(8):
                    ts('dve' if ch % 2 == 0 else 'pool', Wb[:, ch, sl * 225:(sl + 1) * 225], stg[sl % 2][:, ch, :],
                       nw1[:, l * 8 + ch:l * 8 + ch + 1], None, ALU.mult, r=[sk], w=[('Wb', sl, ch)])
            for i in range(2):
                P.add('dve', lambda e, a=vt[i][:, :, 128:129]: e.memset(a, 1.0), [], [('vt1', i)])
                P.add('dve', lambda e, a=mvt[i][:, :, 128:129]: e.memset(a, 1.0), [], [('mvt1', i)])
            for blk in range(8):
                dma('pool', PRE[blk, :, 0:2], zer[:, 0:2])
                dma('pool', PRE[blk, :, S + 2:S + 4], zer[:, 0:2])
            P.barrier()

            cnt = {'bank': 0, 'ev': 0, 'fo': 0, 'pre': 0}

            def nbank():
                k = 2 + cnt['bank'] % 6
                cnt['bank'] += 1
                return k

            def evac_eng():
                cnt['ev'] += 1
                return 'act' if cnt['ev'] % 2 == 0 else 'dve'

            def normA(s_):
                xb = s_ % 2
                xk = 'xt%d' % xb
                dma('sp', xt[xb], xin[s_ * 512:(s_ + 1) * 512, :].rearrange("(s p) d -> p s d", p=128), w=[xk])
                for sub in range(4):
                    c = (s_ % 2) * 4 + sub
                    act(junk, xt[xb][:, sub, :], AF.Square, r=[xk], w=['junk', ('ss', c)], accum_out=smv[:, c:c + 1])
                    ts('dve', smv[:, c:c + 1], smv[:, c:c + 1], 1.0 / D, EPS, ALU.mult, ALU.add, r=[('ss', c)], w=[('ss', c)])
                    rsqrt_chain(smv[:, c:c + 1], 1, ('ss', c))

            def normB(s_):
                xb = s_ % 2
                xk = 'xt%d' % xb
                for sub in range(4):
                    c = (s_ % 2) * 4 + sub
                    nb = sub % 2
                    act(xn[nb], xt[xb][:, sub, :], AF.Copy, r=[xk, ('ss', c)], w=[('xn', nb)], scale=smv[:, c:c + 1])
                    pb = psb(sub % 2)
                    for ch in range(8):
                        tr(pb[:, ch * 128:(ch + 1) * 128], xn[nb][:, ch * 128:(ch + 1) * 128], ident_b,
                           r=[('xn', nb)], w=[('ps', sub % 2)])
                    cp('dve', hT[xb][:, :, sub * 128:(sub + 1) * 128],
                       pb.rearrange("p (c t) -> p c t", c=8), r=[('ps', sub % 2)], w=[('hT', xb, sub)])

            def mains(s_):
                xb = s_ % 2
                t0 = s_ * 512
                hk = [('hT', xb, sub) for sub in range(4)]
                for cb_ in range(16):
                    if cb_ < 4:
                        col0 = cb_ * 128
                    elif cb_ < 8:
                        col0 = 512 + (cb_ - 4) * 128
                    else:
                        col0 = 1536 + (cb_ - 8) * 128
                    bk = nbank()
                    for ch in range(8):
                        mm(ps[bk][:, :], Wb[:, ch, col0:col0 + 128], hT[xb][:, ch, :], ch == 0, ch == 7,
                           r=hk, w=[('ps', bk)])
                    if cb_ < 8:
                        fi = cnt['fo'] % 4
                        cnt['fo'] += 1
                        eng = evac_eng()
                        if cb_ < 4:
                            cp(eng, fo[fi], ps[bk][:, :], r=[('ps', bk)], w=[('fo', fi)])
                            dma('pool', qT[cb_, :, t0:t0 + 512], fo[fi], r=[('fo', fi)])
                        else:
                            if eng == 'act':
                                act(fo[fi], ps[bk][:, :], AF.Copy, r=[('ps', bk)], w=[('fo', fi)], scale=0.125)
                            else:
                                ts('dve', fo[fi], ps[bk][:, :], 0.125, None, ALU.mult, r=[('ps', bk)], w=[('fo', fi)])
                            dma('pool', kT[cb_ - 4, :, t0:t0 + 512], fo[fi], r=[('fo', fi)])
                    else:
                        pi = cnt['pre'] % 3
                        cnt['pre'] += 1
                        cp(evac_eng(), pret[pi], ps[bk][:, :], r=[('ps', bk)], w=[('pre', pi)])
                        dma('pool', PRE[cb_ - 8, :, 2 + t0:2 + t0 + 512], pret[pi], r=[('pre', pi)])
                for sub in range(4):
                    tk = t0 + sub * 128
                    i2 = sub % 2
                    for which, col0 in (('av', 1024), ('mv', 2560), ('mo', 3072)):
                        bk = nbank()
                        for ch in range(8):
                            mm(ps[bk][:, :], hT[xb][:, ch, sub * 128:(sub + 1) * 128], Wb[:, ch, col0:col0 + 512],
                               ch == 0, ch == 7, r=[hk[sub]], w=[('ps', bk)])
                        src = ps[bk][:, :]
                        if which == 'av':
                            cp(evac_eng(), vt[i2][:, :, 0:128], src.rearrange("p (h e) -> p h e", h=4),
                               r=[('ps', bk), ('vt1', i2)], w=[('vt', i2)])
                            dma('pool', Vx[tk:tk + 128, :, :], vt[i2], r=[('vt', i2)])
                        elif which == 'mv':
                            cp(evac_eng(), mvt[i2][:, :, 0:128], src.rearrange("p (h e) -> p h e", h=4),
                               r=[('ps', bk), ('mvt1', i2)], w=[('mvt', i2)])
                            dma('pool', MVx[tk:tk + 128, :, :], mvt[i2], r=[('mvt', i2)])
                        else:
                            cp(evac_eng(), mot[i2], src, r=[('ps', bk)], w=[('mot', i2)])
                            dma('pool', MO[tk:tk + 128, :], mot[i2], r=[('mot', i2)])
                    bk = nbank()
                    for ch in range(8):
                        mm(ps[bk][:, 0:16], hT[xb][:, ch, sub * 128:(sub + 1) * 128], Wb[:, ch, 3584:3600],
                           ch == 0, ch == 7, r=[hk[sub]], w=[('ps', bk)])
                    tt('dve', gtt[i2], ps[bk][:, 0:16], bgb[:, l * 16:(l + 1) * 16], ALU.add,
                       r=[('ps', bk)], w=[('gtt', i2)])
                    dma('pool', G[tk:tk + 128, :], gtt[i2], r=[('gtt', i2)])

            normA(0)
            normB(0)
            for s_ in range(NST):
                if s_ + 1 < NST:
                    normA(s_ + 1)
                mains(s_)
                if s_ + 1 < NST:
                    normB(s_ + 1)
            P.barrier()

        def phase_conv(l, S):
            b = PH
            ptile = [V(b + i * 520, 516) for i in range(3)]; b += 1560
            acc = [V(b + i * 512, 512) for i in range(2)]; b += 1024
            so = [V(b + i * 512, 512) for i in range(2)]; b += 1024
            ob = [V(b + i * 256, 256, BF16) for i in range(2)]; b += 512
            i = 0
            for s_ in range(S // 512):
                t0 = s_ * 512
                for blk in range(8):
                    pi, ai = i % 3, i % 2
                    i += 1
                    dma('sp', ptile[pi], PRE[blk, :, t0:t0 + 516], w=[('pt', pi)])
                    c0 = (l * 8 + blk) * 5
                    ts('dve', acc[ai], ptile[pi][:, 0:512], cwt[:, c0:c0 + 1], None, ALU.mult,
                       r=[('pt', pi)], w=[('acc', ai)])
                    for k in range(1, 5):
                        stt('dve', acc[ai], ptile[pi][:, k:k + 512], cwt[:, c0 + k:c0 + k + 1], acc[ai],
                            ALU.mult, ALU.add, r=[('pt', pi), ('acc', ai)], w=[('acc', ai)])
                    act(so[ai], acc[ai], AF.Silu, r=[('acc', ai)], w=[('so', ai)],
                        bias=cbt[:, l * 8 + blk:l * 8 + blk + 1])
                    if blk < 4:
                        cp('pool', ob[ai], so[ai], r=[('so', ai)], w=[('ob', ai)])
                        dma('pool', mqT[blk * 128:(blk + 1) * 128, t0:t0 + 512], ob[ai], r=[('ob', ai)])
                    else:
                        ts('pool', ob[ai], so[ai], 128.0 ** -0.5, None, ALU.mult, r=[('so', ai)], w=[('ob', ai)])
                        dma('pool', mkT[(blk - 4) * 128:(blk - 3) * 128, t0:t0 + 512], ob[ai], r=[('ob', ai)])
            P.barrier()

        def key_tiles(h, qt, NKT):
            if att_full:
                return list(range(NKT))
            dmin = 40.0 / SLOPES[h]
            lo = max(0, int(math.floor((qt * 512 - dmin) / 128.0)))
            hi = min(NKT - 1, int(math.floor((qt * 512 + 511 + dmin) / 128.0)))
            return list(range(lo, hi + 1))

        def phase_att(l, S):
            NKT = S // 128
            b = PH
            kTs = V(b, S, BF16, "p (m t) -> p m t", m=2); b += S
            Vs = V(b, (NKT * 129 + 1) // 2, BF16)[:, 0:NKT * 129].rearrange("p (k e) -> p k e", k=NKT); b += (NKT * 129 + 1) // 2
            qx = [V(b + i * 1024, 1024, BF16, "p (s m t) -> p s m t", s=2, m=2) for i in range(2)]; b += 2048
            pT2 = [V(b + i * 512, 512, BF16, "p (m t) -> p m t", m=2) for i in range(3)]
            pT = [[pT2[i][:, m, :] for m in range(2)] for i in range(3)]; b += 1536
            ov = V(b, 1032, F32, "p (a e) -> p a e", a=8); b += 1032
            av = [V(b + i * 128, 128) for i in range(2)]; b += 256
            jk = V(b, 128); b += 128
            obf = [V(b + i * 64, 64, BF16) for i in range(2)]; b += 128
            ot = [V(b + i * 256, 256, BF16) for i in range(2)]; b += 512
            sm = V(b, 32); b += 32
            assert b <= ACOLS
            accreg = [(4 + (a // 3), (a % 3) * 129) for a in range(8)]
            qi = 0
            pend = [None]
            for h in range(4):
                kst = min(2048, S)
                kkeys, vkeys = [], []
                for c0 in range(0, S, kst):
                    kkeys.append(('kT', c0))
                    dma('sp', kTs[0:64, :, c0:c0 + kst],
                        kT[h, :, c0:c0 + kst].rearrange("(m d) t -> d m t", m=2), w=[kkeys[-1]])
                for m in range(2):
                    kkeys.append(('kTp', m))
                    dma('sp', kTs[64:68, m, :], d_kpos[h, :, 0:S], w=[kkeys[-1]])
                vst = min(16, NKT)
                for k0 in range(0, NKT, vst):
                    vkeys.append(('Vs', k0))
                    dma('sp', Vs[:, k0:k0 + vst, :],
                        Vx[k0 * 128:(k0 + vst) * 128, h, :].rearrange("(k p) e -> p k e", p=128), w=[vkeys[-1]])
                for qt in range(S // 512):
                    t0 = qt * 512
                    qb = qi % 2
                    qi += 1
                    qk = ('qx', qb)
                    for sg in range(2):
                        dma('sp', qx[qb][0:64, sg, :, :],
                            qT[h, :, t0:t0 + 512].rearrange("(m d) t -> d m t", m=2), w=[qk])
                    for m in range(2):
                        dma('sp', qx[qb][64:68, :, m, :],
                            d_posq[:, :, t0:t0 + 512].rearrange("s r t -> r s t"), w=[qk])
                    kts = key_tiles(h, qt, NKT)
                    n = len(kts)

                    def stageA(i):
                        kt = kts[i]
                        for m in range(2):
                            bk = (i % 2) * 2 + m
                            kl = kTs[0:68, m, kt * 128:(kt + 1) * 128]
                            rr = kkeys + [qk]
                            ww = [('ps', bk)]
                            if kt < 4 * qt:
                                mm(ps[bk][:, :], kl, qx[qb][0:68, 0, m, :], True, True, r=rr, w=ww)
                            elif kt > 4 * qt + 3:
                                mm(ps[bk][:, :], kl, qx[qb][0:68, 1, m, :], True, True, r=rr, w=ww)
                            else:
                                d = kt - 4 * qt
                                if d > 0:
                                    mm(ps[bk][:, 0:128 * d], kl, qx[qb][0:68, 1, m, 0:128 * d], True, True, r=rr, w=ww)
                                mm(ps[bk][:, 128 * d:128 * (d + 1)], kl, qx[qb][0:68, 0, m, 128 * d:128 * (d + 1)],
                                   True, False, r=rr, w=ww)
                                mm(ps[bk][:, 128 * d:128 * (d + 1)], ident_b, dcorr[h], False, True, r=rr, w=ww)
                                if d < 3:
                                    mm(ps[bk][:, 128 * (d + 1):512], kl, qx[qb][0:68, 0, m, 128 * (d + 1):512],
                                       True, True, r=rr, w=ww)

                    def stageB(i):
                        b0 = (i % 2) * 2
                        act(pT2[i % 3], psq[:, b0:b0 + 2, :], AF.Exp, r=[('ps', b0), ('ps', b0 + 1)],
                            w=[('pT', i % 3, 0), ('pT', i % 3, 1)])

                    def stageC(i):
                        kt = kts[i]
                        for m in range(2):
                            for sub in range(4):
                                bk, c0 = accreg[m * 4 + sub]
                                a_ = m * 4 + sub
                                mm(ps[bk][:, c0:c0 + 129], pT[i % 3][m][:, sub * 128:(sub + 1) * 128], Vs[:, kt, :],
                                   i == 0 and a_ % 3 == 0, i == n - 1 and (a_ % 3 == 2 or a_ == 7),
                                   r=[('pT', i % 3, m)] + vkeys, w=[('acc', bk)])

                    stageA(0)
                    if n > 1:
                        stageA(1)
                    for i in range(n):
                        stageB(i)
                        stageC(i)
                        if i + 2 < n:
                            stageA(i + 2)
                        if i == 1 and pend[0] is not None:
                            pend[0]()
                            pend[0] = None
                    for bk in range(4, 7):
                        na = 3 if bk < 6 else 2
                        cp('dve', ov[:, (bk - 4) * 3:(bk - 4) * 3 + na, :],
                           ps[bk][:, 0:na * 129].rearrange("p (a e) -> p a e", a=na),
                           r=[('acc', bk)], w=[('ov', bk)])
                    ovk = [('ov', 4), ('ov', 5), ('ov', 6)]
                    rinv = sm[:, 0:8]
                    P.add('dve', lambda e, rinv=rinv, ov=ov: e.reciprocal(out=rinv.unsqueeze(2), in_=ov[:, :, 128:129]),
                          ovk, ['rinv'])
                    ts('dve', sm[:, 8:12], sm[:, 4:8], nlam[:, l:l + 1], None, ALU.mult, r=['rinv'], w=['rl'])
                    oi = qi % 2

                    def part2(h=h, t0=t0, oi=oi):
                        pb = psb(7)
                        for sub in range(4):
                            tr(pb[:, sub * 128:(sub + 1) * 128], obf_all[oi][:, sub * 128:(sub + 1) * 128], ident_b,
                               r=[('obf', oi)], w=[('ps', 7)])
                        cp('act', ot[oi], pb[:, 0:512], r=[('ps', 7)], w=[('ot', oi)])
                        dma('pool', mixT[h * 128:(h + 1) * 128, t0:t0 + 512], ot[oi], r=[('ot', oi)])

                    for sub in range(4):
                        ai = sub % 2
                        ts('dve', av[ai], ov[:, sub, 0:128], sm[:, sub:sub + 1], None, ALU.mult,
                           r=ovk + ['rinv'], w=[('av', ai)])
                        stt('dve', av[ai], ov[:, 4 + sub, 0:128], sm[:, 8 + sub:9 + sub], av[ai], ALU.mult, ALU.add,
                            r=ovk + ['rl', ('av', ai)], w=[('av', ai)])
                        sk = ('ssq', sub)
                        act(jk, av[ai], AF.Square, r=[('av', ai)], w=['jk', sk], accum_out=sm[:, 16 + sub:17 + sub])
                        ts('dve', sm[:, 16 + sub:17 + sub], sm[:, 16 + sub:17 + sub], 1.0 / 128, EPS, ALU.mult, ALU.add,
                           r=[sk], w=[sk])
                        rsqrt_chain(sm[:, 16 + sub:17 + sub], 1, sk)
                        stt('dve', obf_all[oi][:, sub * 128:(sub + 1) * 128], av[ai], sm[:, 16 + sub:17 + sub],
                            anwb[:, l * 128:(l + 1) * 128], ALU.mult, ALU.mult, r=[('av', ai), sk], w=[('obf', oi)])
                    pend[0] = part2
            if pend[0] is not None:
                pend[0]()
                pend[0] = None
            P.barrier()

        obf_all = [V(ACOLS - 512 + i * 256, 256, BF16) for i in range(2)]

        def phase_mlstm(l, S):
            NC = S // 128
            b = PH

            def A(n, dt=F32, pat=None, **kw):
                nonlocal b
                v = V(b, n, dt, pat, **kw)
                b += n
                return v

            def A2(n, dt=F32, pat=None, **kw):
                return [A(n, dt, pat, **kw) for _ in range(2)]
            qc = A2(256, BF16, "p (h t) -> p h t", h=4)
            kc = A2(256, BF16, "p (h t) -> p h t", h=4)
            vx = A2(258, BF16, "p (h e) -> p h e", h=4)
            gt = A2(16)
            mo = A2(512)
            hfl = A2(512)
            g4 = A2(32)
            pe16 = A2(16)
            ex = A2(16)
            vs1 = A2(258, BF16, "p (h e) -> p h e", h=4)
            vs2 = A2(258, BF16, "p (h e) -> p h e", h=4)
            kk = A2(256, BF16, "p (h d) -> p h d", h=4)
            smk = A2(256, BF16, "p (h j) -> p h j", h=4)
            nd = A(516, F32, "p (h e) -> p h e", h=4)
            dd = A(16)
            hd = A2(512, F32, "p (h e) -> p h e", h=4)
            Cst = A(516, F32, "p (h e) -> p h e", h=4)
            Cb = A(258, BF16, "p (h e) -> p h e", h=4)
            hs = A(512, F32, "p (h e) -> p h e", h=4)
            sq = A(512, F32, "p (h e) -> p h e", h=4)
            og = A(512, F32, "p (h e) -> p h e", h=4)
            memb = A(256, BF16, "p (h e) -> p h e", h=4)
            mst = A2(1024, BF16, "p (h t) -> p h t", h=4)
            assert b <= ACOLS - 512
            for d in range(2):
                P.add('dve', lambda e: e.memset(Cst, 0.0), [], ['Cst'])
                P.add('pool', lambda e: e.memset(Cb, 0.0), [], ['Cb'])
                order = list(range(NC)) if d == 0 else list(range(NC - 1, -1, -1))

                def stage1(ii, d=d, order=order):
                    c = order[ii]
                    t0 = c * 128
                    bi = ii % 2
                    dma('sp', qc[bi], mqT[:, t0:t0 + 128].rearrange("(h d) t -> d h t", h=4), w=[('qc', bi)])
                    dma('sp', kc[bi], mkT[:, t0:t0 + 128].rearrange("(h d) t -> d h t", h=4), w=[('kc', bi)])
                    dma('sp', vx[bi], MVx[t0:t0 + 128, :, :], w=[('vx', bi)])
                    dma('sp', gt[bi], G[t0:t0 + 128, :], w=[('gt', bi)])
                    if d == 1:
                        dma('sp', mo[bi], MO[t0:t0 + 128, :], w=[('mo', bi)])
                        dma('sp', hfl[bi], HF[t0:t0 + 128, :], w=[('hfl', bi)])
                    fg = gt[bi][:, 8 + 4 * d:12 + 4 * d]
                    ig = gt[bi][:, 4 * d:4 * d + 4]
                    g_ = g4[bi]
                    ab, t1, l1, lf = g_[:, 0:4], g_[:, 4:8], g_[:, 8:12], g_[:, 12:16]
                    stt('dve', ab, fg, -1.0, fg, ALU.mult, ALU.max, r=[('gt', bi)], w=[('ab', bi)])
                    act(t1, ab, AF.Exp, r=[('ab', bi)], w=[('t1', bi)], scale=-1.0)
                    act(l1, t1, AF.Ln, r=[('t1', bi)], w=[('l1', bi)], bias=1.0)
                    stt('dve', lf, fg, 0.0, l1, ALU.min, ALU.subtract, r=[('gt', bi), ('l1', bi)], w=[('lf', bi)])
                    mm(ps[0][:, 0:4], tri[d], lf, True, True, r=[('lf', bi)], w=[('ps', 0)])
                    mm(ps[0][:, 4:8], ones_f, lf, True, True, r=[('lf', bi)], w=[('ps', 0)])
                    p16 = pe16[bi]
                    pk = ('pe16', bi)
                    tt('dve', p16[:, 0:4], ig, ps[0][:, 0:4], ALU.subtract, r=[('gt', bi), ('ps', 0)], w=[pk])
                    cp('dve', p16[:, 4:8], ps[0][:, 0:4], r=[('ps', 0)], w=[pk])
                    tt('dve', p16[:, 8:12], p16[:, 0:4], ps[0][:, 4:8], ALU.add, r=[pk, ('ps', 0)], w=[pk])
                    cp('dve', p16[:, 12:16], ps[0][:, 4:8], r=[('ps', 0)], w=[pk])
                    ek = ('ex', bi)
                    act(ex[bi], p16, AF.Exp, r=[pk], w=[ek])
                    tt('dve', vs1[bi], vx[bi], ex[bi][:, 0:4].unsqueeze(2).to_broadcast([128, 4, 129]), ALU.mult,
                       r=[('vx', bi), ek], w=[('vs1', bi)])
                    tt('pool', vs2[bi], vx[bi], ex[bi][:, 8:12].unsqueeze(2).to_broadcast([128, 4, 129]), ALU.mult,
                       r=[('vx', bi), ek], w=[('vs2', bi)])
                    pkk = psb(1)
                    for h in range(4):
                        tr(pkk[:, h * 128:(h + 1) * 128], kc[bi][:, h, :], ident_b, r=[('kc', bi)], w=[('ps', 1)])
                    cp('act', kk[bi], pkk[:, 0:512].rearrange("p (h d) -> p h d", h=4), r=[('ps', 1)], w=[('kk', bi)])
                    for h in range(4):
                        mm(ps[2][:, h * 128:(h + 1) * 128], kc[bi][:, h, :], qc[bi][:, h, :], True, True,
                           r=[('kc', bi), ('qc', bi)], w=[('ps', 2)])
                    tt('dve', smk[bi], ps[2][:, :].rearrange("p (h j) -> p h j", h=4),
                       msk[d].rearrange("p (h j) -> p h j", h=4), ALU.mult, r=[('ps', 2)], w=[('smk', bi)])

                def stage2(ii, d=d, order=order):
                    c = order[ii]
                    t0 = c * 128
                    bi = ii % 2
                    ek = ('ex', bi)
                    for h in range(4):
                        bk = 5 + h // 2
                        c0 = (h % 2) * 129
                        mm(ps[bk][:, c0:c0 + 129], kk[bi][:, h, :], vs2[bi][:, h, :], True, True,
                           r=[('kk', bi), ('vs2', bi)], w=[('ps', bk)])
                    for h in range(4):
                        bk = 3 + h // 2
                        c0 = (h % 2) * 129
                        mm(ps[bk][:, c0:c0 + 129], qc[bi][:, h, :], Cb[:, h, :], True, False,
                           r=[('qc', bi), 'Cb'], w=[('ps', bk)])
                        mm(ps[bk][:, c0:c0 + 129], smk[bi][:, h, :], vs1[bi][:, h, :], False, True,
                           r=[('smk', bi), ('vs1', bi)], w=[('ps', bk)])
                    for h in range(4):
                        bk = 5 + h // 2
                        c0 = (h % 2) * 129
                        stt('dve', Cst[:, h, :], Cst[:, h, :], ex[bi][:, 12 + h:13 + h], ps[bk][:, c0:c0 + 129],
                            ALU.mult, ALU.add, r=['Cst', ek, ('ps', bk)], w=['Cst'])
                    cp('pool', Cb, Cst, r=['Cst'], w=['Cb'])
                    for half in range(2):
                        tt('dve', nd[:, 2 * half:2 * half + 2, :],
                           ps[3 + half][:, 0:258].rearrange("p (h e) -> p h e", h=2),
                           ex[bi][:, 4 + 2 * half:6 + 2 * half].unsqueeze(2).to_broadcast([128, 2, 129]), ALU.mult,
                           r=[('ps', 3 + half), ek], w=[('nd', half)])
                    ndk = [('nd', 0), ('nd', 1)]
                    stt('dve', dd[:, 0:4].unsqueeze(2), nd[:, :, 128:129], -1.0, nd[:, :, 128:129], ALU.mult, ALU.max,
                        r=ndk, w=['dd'])
                    ts('dve', dd[:, 0:4], dd[:, 0:4], 1.0, None, ALU.max, r=['dd'], w=['dd'])
                    P.add('dve', lambda e, dd=dd: e.reciprocal(out=dd[:, 4:8], in_=dd[:, 0:4]), ['dd'], ['rd'])
                    hi = ii % 2
                    tt('dve', hd[hi], nd[:, :, 0:128], dd[:, 4:8].unsqueeze(2).to_broadcast([128, 4, 128]), ALU.mult,
                       r=ndk + ['rd'], w=[('hd', hi)])
                    if d == 0:
                        dma('pool', HF[t0:t0 + 128, :], hd[hi].rearrange("p h e -> p (h e)"), r=[('hd', hi)])
                    else:
                        tt('dve', hs, hd[hi], hfl[bi].rearrange("p (h e) -> p h e", h=4), ALU.add,
                           r=[('hd', hi), ('hfl', bi)], w=['hs'])
                        tt('pool', sq, hs, hs, ALU.mult, r=['hs'], w=['sq'])
                        P.add('dve', lambda e, dd=dd, sq=sq: e.tensor_reduce(out=dd[:, 8:12], in_=sq, axis=AX.X, op=ALU.add),
                              ['sq'], ['ss4'])
                        ts('dve', dd[:, 8:12], dd[:, 8:12], 1.0 / 128, EPS, ALU.mult, ALU.add, r=['ss4'], w=['ss4'])
                        rsqrt_chain(dd[:, 8:12], 4, 'ss4')
                        tt('dve', hs, hs, dd[:, 8:12].unsqueeze(2).to_broadcast([128, 4, 128]), ALU.mult,
                           r=['hs', 'ss4'], w=['hs'])
                        tt('pool', hs, hs, mnwb[:, l * 512:(l + 1) * 512].rearrange("p (h e) -> p h e", h=4), ALU.mult,
                           r=['hs'], w=['hs'])
                        act(og, mo[bi].rearrange("p (h e) -> p h e", h=4), AF.Sigmoid, r=[('mo', bi)], w=['og'])
                        tt('dve', memb, og, hs, ALU.mult, r=['og', 'hs'], w=['memb'])
                        pm = psb(7)
                        for h in range(4):
                            tr(pm[:, h * 128:(h + 1) * 128], memb[:, h, :], ident_b, r=['memb'], w=[('ps', 7)])
                        grp = c // 4
                        gi = grp % 2
                        cp('act', mst[gi][:, :, (c % 4) * 128:(c % 4 + 1) * 128],
                           pm[:, 0:512].rearrange("p (h t) -> p h t", h=4), r=[('ps', 7)], w=[('mst', gi)])
                        if c % 4 == 0:
                            dma('pool', mixT[512:1024, grp * 512:(grp + 1) * 512].rearrange("(h e) t -> e h t", h=4),
                                mst[gi], r=[('mst', gi)])

                stage1(0)
                for ii in range(NC):
                    if ii + 1 < NC:
                        stage1(ii + 1)
                    stage2(ii)
                P.barrier()

        def phase3(l, S, xin, xout, final):
            NT = S // 256
            b = PH
            Wo = V(b, 4096, BF16, "p (c n) -> p c n", c=8); b += 4096
            W1 = V(b, 22528, BF16, "p (c n) -> p c n", c=8); b += 22528
            W2 = V(b, 11264, BF16, "p (c n) -> p c n", c=22); b += 11264
            xt = V(b, 2048, F32, "p (s d) -> p s d", s=2); b += 2048
            sb0 = b
            mx = V(b, 1024, BF16, "p (c t) -> p c t", c=8); b += 1024
            hT = V(b, 1024, BF16, "p (c t) -> p c t", c=8); b += 1024
            aT = V(b, 2816, BF16, "p (c t) -> p c t", c=22); b += 2816
            xn = V(b, 512, BF16); b += 512
            junk = V(b, 512, BF16); b += 512
            sg = [V(b + i * 256, 256) for i in range(2)]; b += 512
            smv = V(b, 16); b += 16
            assert b <= ACOLS - 512, b
            stg = [V(sb0 + i * 2048, 2048, F32, "p (c n) -> p c n", c=8) for i in range(2)]
            stg2 = [V(sb0 + i * 2048, 2048, F32, "p (c n) -> p c n", c=2) for i in range(2)]
            si = 0
            wv = w_out[l].rearrange("(c p) n -> p c n", p=128)
            for sl in range(4):
                sk = 'stg%d' % (si % 2)
                dma('sp', stg[si % 2], wv[:, :, sl * 256:(sl + 1) * 256], w=[sk])
                for ch in range(8):
                    cp('dve' if ch % 2 == 0 else 'pool', Wo[:, ch, sl * 256:(sl + 1) * 256], stg[si % 2][:, ch, :],
                       r=[sk], w=[('Wo', sl, ch)])
                si += 1
            wv = w_f1[l].rearrange("(c p) n -> p c n", p=128)
            for sl in range(22):
                sk = 'stg%d' % (si % 2)
                dma('sp', stg[si % 2], wv[:, :, sl * 256:(sl + 1) * 256], w=[sk])
                for ch in range(8):
                    ts('dve' if ch % 2 == 0 else 'pool', W1[:, ch, sl * 256:(sl + 1) * 256], stg[si % 2][:, ch, :],
                       nw2[:, l * 8 + ch:l * 8 + ch + 1], None, ALU.mult, r=[sk], w=[('W1', sl, ch)])
                si += 1
            wv = w_f2[l].rearrange("(c p) n -> p c n", p=128)
            for sl in range(11):
                sk = 'stg%d' % (si % 2)
                dma('sp', stg2[si % 2], wv[:, 2 * sl:2 * sl + 2, :], w=[sk])
                for ch in range(2):
                    cp('dve' if ch % 2 == 0 else 'pool', W2[:, 2 * sl + ch, :], stg2[si % 2][:, ch, :],
                       r=[sk], w=[('W2', sl, ch)])
                si += 1
            P.barrier()
            cnt = {'bank': 0}

            def nbank():
                k = 1 + cnt['bank'] % 7
                cnt['bank'] += 1
                return k

            for it in range(NT):
                t0 = it * 256
                dma('sp', xt, xin[t0:t0 + 256, :].rearrange("(s p) d -> p s d", p=128), w=['xt'])
                dma('sp', mx, mixT[:, t0:t0 + 256].rearrange("(c p) t -> p c t", p=128), w=['mx'])
                for sub in range(2):
                    for n in range(2):
                        bk = nbank()
                        for ch in range(8):
                            mm(ps[bk][:, :], mx[:, ch, sub * 128:(sub + 1) * 128], Wo[:, ch, n * 512:(n + 1) * 512],
                               ch == 0, ch == 7, r=['mx'], w=[('ps', bk)])
                        tt('dve', xt[:, sub, n * 512:(n + 1) * 512], xt[:, sub, n * 512:(n + 1) * 512], ps[bk][:, :],
                           ALU.add, r=[('ps', bk), 'xt'], w=['xt'])
                for sub in range(2):
                    act(junk, xt[:, sub, :], AF.Square, r=['xt'], w=['junk', ('ss', sub)], accum_out=smv[:, sub:sub + 1])
                    ts('dve', smv[:, sub:sub + 1], smv[:, sub:sub + 1], 1.0 / D, EPS, ALU.mult, ALU.add,
                       r=[('ss', sub)], w=[('ss', sub)])
                    rsqrt_chain(smv[:, sub:sub + 1], 1, ('ss', sub))
                    act(xn, xt[:, sub, :], AF.Copy, r=['xt', ('ss', sub)], w=['xn'], scale=smv[:, sub:sub + 1])
                    pb = psb(0)
                    for ch in range(8):
                        tr(pb[:, ch * 128:(ch + 1) * 128], xn[:, ch * 128:(ch + 1) * 128], ident_b, r=['xn'], w=[('ps', 0)])
                    cp('dve', hT[:, :, sub * 128:(sub + 1) * 128], pb.rearrange("p (c t) -> p c t", c=8),
                       r=[('ps', 0)], w=[('hT', sub)])
                hk = [('hT', 0), ('hT', 1)]
                for j in range(22):
                    ba = nbank()
                    for ch in range(8):
                        mm(ps[ba][:, 0:256], W1[:, ch, j * 128:(j + 1) * 128], hT[:, ch, :], ch == 0, ch == 7,
                           r=hk, w=[('ps', ba)])
                    bb = nbank()
                    for ch in range(8):
                        mm(ps[bb][:, 0:256], W1[:, ch, FF + j * 128:FF + (j + 1) * 128], hT[:, ch, :], ch == 0, ch == 7,
                           r=hk, w=[('ps', bb)])
                    act(sg[j % 2], ps[ba][:, 0:256], AF.Silu, r=[('ps', ba)], w=[('sg', j % 2)])
                    tt('dve', aT[:, j, :], sg[j % 2], ps[bb][:, 0:256], ALU.mult, r=[('sg', j % 2), ('ps', bb)],
                       w=[('aT', j)])
                ak = [('aT', j) for j in range(22)]
                for sub in range(2):
                    for n in range(2):
                        bk = nbank()
                        for j in range(22):
                            mm(ps[bk][:, :], aT[:, j, sub * 128:(sub + 1) * 128], W2[:, j, n * 512:(n + 1) * 512],
                               j == 0, j == 21, r=ak, w=[('ps', bk)])
                        tt('dve', xt[:, sub, n * 512:(n + 1) * 512], xt[:, sub, n * 512:(n + 1) * 512], ps[bk][:, :],
                           ALU.add, r=[('ps', bk), 'xt'], w=['xt'])
                if final:
                    for sub in range(2):
                        c = 4 + sub
                        act(junk, xt[:, sub, :], AF.Square, r=['xt'], w=['junk', ('ss', c)], accum_out=smv[:, c:c + 1])
                        ts('dve', smv[:, c:c + 1], smv[:, c:c + 1], 1.0 / D, EPS, ALU.mult, ALU.add,
                           r=[('ss', c)], w=[('ss', c)])
                        rsqrt_chain(smv[:, c:c + 1], 1, ('ss', c))
                        stt('dve', xt[:, sub, :], xt[:, sub, :], smv[:, c:c + 1], fnwb, ALU.mult, ALU.mult,
                            r=['xt', ('ss', c)], w=['xt'])
                dma('pool', xout[t0:t0 + 256, :].rearrange("(s p) d -> p s d", p=128), xt, r=['xt'])
            P.barrier()

        for si_, S in enumerate(seq_lens):
            for l in range(nlayers):
                xin = xs[si_] if l == 0 else X1[0:S, :]
                phase1(l, S, xin)
                phase_conv(l, S)
                phase_att(l, S)
                phase_mlstm(l, S)
                phase3(l, S, xin, ys[si_] if l == 1 else X1[0:S, :], l == 1)
        P.emit(block)
    return nc


def host_consts(Smax):
    cf = np.zeros((128, 1536), np.float32)
    a = np.arange(128)
    cf[:, 0:128] = np.eye(128)
    cf[:, 128:256] = 1.0
    le = (a[:, None] <= a[None, :]).astype(np.float32)
    ge = (a[:, None] >= a[None, :]).astype(np.float32)
    cf[:, 256:384] = le
    cf[:, 384:512] = ge
    cf[:, 512:1024] = np.tile(le, (1, 4))
    cf[:, 1024:1536] = np.tile(ge, (1, 4))
    cb = np.zeros((128, 640), np.float32)
    cb[:, 0:128] = np.eye(128)
    for h in range(4):
        cb[:, 128 + 128 * h:256 + 128 * h] = -2.0 * SLOPES[h] * np.maximum(a[:, None] - a[None, :], 0)
    t = np.arange(Smax)
    ta, tb = (t // 128).astype(np.float32), (t % 128).astype(np.float32)
    posq = np.zeros((2, 4, Smax), np.float32)
    for s, sg in enumerate((1.0, -1.0)):
        posq[s, 0] = sg * ta
        posq[s, 1] = sg * tb
        posq[s, 2] = sg
        posq[s, 3] = sg
    kpos = np.zeros((4, 4, Smax), np.float32)
    for h in range(4):
        kpos[h, 0] = -SLOPES[h] * 128
        kpos[h, 1] = -SLOPES[h]
        kpos[h, 2] = SLOPES[h] * 128 * ta
        kpos[h, 3] = SLOPES[h] * tb
    return {"c_f32": cf, "c_bf16": cb.astype(NBF), "posq": posq.astype(NBF), "kpos": kpos.astype(NBF)}


def host_weights(norm1_w, w_in, b_gate, conv_w, conv_b, lam, att_norm_w, mlstm_norm_w, w_out, norm2_w,
                 w_ffn_in, w_ffn_out, final_norm_w):
    f = lambda a: np.ascontiguousarray(np.asarray(a, np.float32))
    m = {}
    m["w_in"] = f(w_in)
    m["w_out"] = f(w_out)
    m["w_ffn_in"] = f(w_ffn_in)
    m["w_ffn_out"] = f(w_ffn_out)
    m["norm1_w"] = f(np.asarray(norm1_w).reshape(2, 8, 128).transpose(2, 0, 1).reshape(128, 16))
    m["norm2_w"] = f(np.asarray(norm2_w).reshape(2, 8, 128).transpose(2, 0, 1).reshape(128, 16))
    m["final_norm_w"] = f(np.asarray(final_norm_w).reshape(1, D))
    m["b_gate"] = f(np.asarray(b_gate).reshape(1, 32))
    m["conv_w"] = f(np.asarray(conv_w).reshape(2, 5, 8, 128).transpose(3, 0, 2, 1).reshape(128, 80))
    m["conv_b"] = f(np.asarray(conv_b).reshape(2, 8, 128).transpose(2, 0, 1).reshape(128, 16))
    m["lam"] = f(np.asarray(lam).reshape(1, 512))
    m["att_norm_w"] = f(np.asarray(att_norm_w).reshape(1, 256))
    m["mlstm_norm_w"] = f(np.asarray(mlstm_norm_w).reshape(1, 1024))
    return m


_CACHE = {}


def kernel(x_prompt, x_sample, norm1_w, w_in, b_gate, conv_w, conv_b, lam, att_norm_w, mlstm_norm_w,
           w_out, norm2_w, w_ffn_in, w_ffn_out, final_norm_w):
    x_prompt = np.asarray(x_prompt, np.float32)
    x_sample = np.asarray(x_sample, np.float32)
    B, S, _ = x_prompt.shape
    DB, DS, _ = x_sample.shape
    n = 8
    key = (S, DS)
    if key not in _CACHE:
        _CACHE[key] = build([S, DS])
    nc = _CACHE[key]
    base = host_weights(norm1_w, w_in, b_gate, conv_w, conv_b, lam, att_norm_w, mlstm_norm_w, w_out, norm2_w,
                        w_ffn_in, w_ffn_out, final_norm_w)
    base.update(host_consts(max(S, DS)))
    in_maps = []
    for c in range(n):
        m = dict(base)
        m["x0"] = np.ascontiguousarray(x_prompt[c])
        m["x1"] = np.ascontiguousarray(x_sample[c // 4])
        in_maps.append(m)
    res = run_bass_kernel_spmd(nc, in_maps, core_ids=list(range(n)))
    y_prompt = np.stack([np.asarray(res.results[c]["y0"], np.float32) for c in range(n)], axis=0)
    q = DS // 4
    y_sample = np.zeros((DB, DS, D), np.float32)
    for c in range(n):
        y_sample[c // 4, (c % 4) * q:(c % 4 + 1) * q] = np.asarray(res.results[c]["y1"], np.float32)[(c % 4) * q:(c % 4 + 1) * q]
    return (y_prompt, y_sample)
```

```python
import math
from contextlib import ExitStack

import numpy as np
import ml_dtypes
import concourse.bass as bass
import concourse.mybir as mybir
from concourse.bass_utils import run_bass_kernel_spmd

F32 = mybir.dt.float32
BF16 = mybir.dt.bfloat16
AF = mybir.ActivationFunctionType
ALU = mybir.AluOpType
AX = mybir.AxisListType

D = 1024
FF = 2816
EPS = 1e-6
NSLOT = 8
SAME_SYNC = True
ACOLS = 52800
PH = 4864
NBF = ml_dtypes.bfloat16
SLOPES = [2.0 ** (-2.0 * (h + 1)) for h in range(4)]


class Prog:
    def __init__(self, nc, stack):
        self.nc = nc
        self.engs = ['pe', 'act', 'dve', 'pool', 'sp']
        self.ops = {e: [] for e in self.engs}
        self.lastw = {}
        self.rd_e = {}
        self.rd_d = {}
        self.ndma = {e: 0 for e in self.engs}
        self.esem = {e: stack.enter_context(nc.semaphore("s_" + e)) for e in self.engs}
        self.dsem = {e: [stack.enter_context(nc.semaphore("d_%s%d" % (e, i))) for i in range(NSLOT)]
                     for e in ('sp', 'pool')}
        self.pending = {e: set() for e in self.engs}

    def add(self, eng, fn, r=(), w=(), dma=False):
        deps = set(self.pending[eng])
        self.pending[eng] = set()
        for b in r:
            t = self.lastw.get(b)
            if t is not None:
                deps.add(t)
        for b in w:
            t = self.lastw.get(b)
            if t is not None:
                deps.add(t)
            for e2, i2 in self.rd_e.get(b, {}).items():
                deps.add(('e', e2, i2))
            for t2 in self.rd_d.get(b, ()):
                deps.add(t2)
        idx = len(self.ops[eng])
        if dma:
            n = self.ndma[eng]
            self.ndma[eng] += 1
            tok = ('d', eng, n)
            if n >= NSLOT:
                deps.add(('d', eng, n - NSLOT))
        else:
            tok = ('e', eng, idx)
        deps = {t for t in deps
                if not (t[0] == 'e' and t[1] == eng and not dma and (eng == 'pe' or not SAME_SYNC))}
        for b in r:
            if dma:
                self.rd_d.setdefault(b, []).append(tok)
            else:
                self.rd_e.setdefault(b, {})[eng] = idx
        for b in w:
            self.lastw[b] = tok
            self.rd_e[b] = {}
            self.rd_d[b] = []
        self.ops[eng].append([fn, deps, tok, False])
        return tok

    def barrier(self):
        toks = set()
        for e in self.engs:
            for op in reversed(self.ops[e]):
                if op[2][0] == 'e':
                    toks.add(op[2])
                    break
            n = self.ndma[e]
            for k in range(max(0, n - NSLOT), n):
                toks.add(('d', e, k))
        for e in self.engs:
            self.pending[e] |= toks
        self.lastw = {}
        self.rd_e = {}
        self.rd_d = {}

    def emit(self, block):
        for e in self.engs:
            for op in self.ops[e]:
                for t in op[1]:
                    if t[0] == 'e':
                        self.ops[t[1]][t[2]][3] = True
        val = {}
        last = {e: 0 for e in self.engs}
        for e in self.engs:
            c = 0
            for i, op in enumerate(self.ops[e]):
                if op[3]:
                    c += 1
                    val[(e, i)] = c
            last[e] = c

        def semval(t):
            if t[0] == 'e':
                return ('e', t[1]), self.esem[t[1]], val[(t[1], t[2])]
            return (('d', t[1], t[2] % NSLOT), self.dsem[t[1]][t[2] % NSLOT],
                    16 * (t[2] // NSLOT + 1))

        def run(e, eo):
            waited = {}
            for fn, deps, tok, ms in self.ops[e]:
                need = {}
                for t in deps:
                    k, s, v = semval(t)
                    if waited.get(k, 0) < v and need.get(k, (None, 0))[1] < v:
                        need[k] = (s, v)
                for k, (s, v) in need.items():
                    eo.wait_ge(s, v)
                    waited[k] = v
                ins = fn(eo)
                if tok[0] == 'd':
                    ins.then_inc(self.dsem[e][tok[2] % NSLOT], 16)
                elif ms:
                    ins.then_inc(self.esem[e], 1)
            if e == 'sp':
                for q in ('sp', 'pool'):
                    n = self.ndma[q]
                    for k in range(max(0, n - NSLOT), n):
                        kk, s, v = semval(('d', q, k))
                        if waited.get(kk, 0) < v:
                            eo.wait_ge(s, v)
                            waited[kk] = v
                for q in self.engs:
                    if q != 'sp' and last[q] > 0:
                        eo.wait_ge(self.esem[q], last[q])

        @block.tensor
        def _(eo):
            run('pe', eo)

        @block.scalar
        def _(eo):
            run('act', eo)

        @block.vector
        def _(eo):
            run('dve', eo)

        @block.gpsimd
        def _(eo):
            run('pool', eo)

        @block.sync
        def _(eo):
            run('sp', eo)


def build(seq_lens, dbg=False, att_full=False, nlayers=2):
    nc = bass.Bass("TRN2", target_bir_lowering=False)
    Smax = max(seq_lens)

    def din(name, shape, dt=F32):
        return nc.dram_tensor(name, list(shape), dt, kind="ExternalInput").ap()

    def dscr(name, shape, dt):
        return nc.dram_tensor(name, list(shape), dt,
                              kind=("ExternalOutput" if dbg else "Internal")).ap()

    xs = [din("x%d" % i, [S, D]) for i, S in enumerate(seq_lens)]
    ys = [nc.dram_tensor("y%d" % i, [S, D], F32, kind="ExternalOutput").ap()
          for i, S in enumerate(seq_lens)]
    w_in = din("w_in", [2, D, 3600])
    w_out = din("w_out", [2, D, D])
    w_f1 = din("w_ffn_in", [2, D, 2 * FF])
    w_f2 = din("w_ffn_out", [2, FF, D])
    d_nw1 = din("norm1_w", [128, 16])
    d_nw2 = din("norm2_w", [128, 16])
    d_fnw = din("final_norm_w", [1, D])
    d_bg = din("b_gate", [1, 32])
    d_cw = din("conv_w", [128, 80])
    d_cb = din("conv_b", [128, 16])
    d_lam = din("lam", [1, 512])
    d_anw = din("att_norm_w", [1, 256])
    d_mnw = din("mlstm_norm_w", [1, 1024])
    d_cf = din("c_f32", [128, 1536])
    d_cbf = din("c_bf16", [128, 640], BF16)
    d_posq = din("posq", [2, 4, Smax], BF16)
    d_kpos = din("kpos", [4, 4, Smax], BF16)

    class _Scr:
        pass
    SCS = []
    for si_, S_ in enumerate(seq_lens):
        o_ = _Scr()
        sfx = "_%d" % si_
        o_.qT = dscr("s_qT" + sfx, [4, 128, S_], BF16)
        o_.kT = dscr("s_kT" + sfx, [4, 128, S_], BF16)
        o_.Vx = dscr("s_Vx" + sfx, [S_, 4, 129], BF16)
        o_.MVx = dscr("s_MVx" + sfx, [S_, 4, 129], BF16)
        o_.MO = dscr("s_MO" + sfx, [S_, 512], F32)
        o_.G = dscr("s_G" + sfx, [S_, 16], F32)
        o_.PRE = dscr("s_pre" + sfx, [8, 128, S_ + 4], F32)
        o_.mqT = dscr("s_mqT" + sfx, [512, S_], BF16)
        o_.mkT = dscr("s_mkT" + sfx, [512, S_], BF16)
        o_.HF = dscr("s_HF" + sfx, [S_, 512], F32)
        o_.mixT = dscr("s_mixT" + sfx, [1024, S_], BF16)
        o_.X1 = dscr("s_X1" + sfx, [S_, D], F32)
        SCS.append(o_)
    cur = [SCS[0]]

    with ExitStack() as st:
        P = Prog(nc, st)
        arena = st.enter_context(nc.sbuf_tensor("arena", [128, ACOLS], F32))
        psq = st.enter_context(nc.psum_tensor("psq", [128, 4, 512], F32))
        ps = [psq[:, i, :] for i in range(4)] + \
             [st.enter_context(nc.psum_tensor("ps%d" % i, [128, 512], F32))[:, :] for i in range(4, 8)]
        block = st.enter_context(nc.Block())

        def V(off, n, dt=F32, pat=None, **kw):
            ap = arena[:, off:off + n]
            if dt is BF16:
                ap = ap.bitcast(BF16)
            if pat:
                ap = ap.rearrange(pat, **kw)
            return ap

        def psb(i):
            return ps[i][:, :].bitcast(BF16)

        def dma(q, out, in_, r=(), w=()):
            P.add(q, lambda e: e.dma_start(out=out, in_=in_), r, w, dma=True)

        def mm(out, lhsT, rhs, start, stop, r=(), w=()):
            P.add('pe', lambda e: e.matmul(out, lhsT=lhsT, rhs=rhs, start=start, stop=stop), r, w)

        def tr(out, in_, ident, r=(), w=()):
            P.add('pe', lambda e: e.transpose(out, in_, ident), r, w)

        def act(out, in_, func, r=(), w=(), **kw):
            P.add('act', lambda e: e.activation(out=out, in_=in_, func=func, **kw), r, w)

        def cp(eng, out, in_, r=(), w=()):
            if eng == 'act':
                P.add('act', lambda e: e.copy(out=out, in_=in_), r, w)
            else:
                P.add(eng, lambda e: e.tensor_copy(out=out, in_=in_), r, w)

        def ts(eng, out, in0, s1, s2, op0, op1=None, r=(), w=()):
            if op1 is None:
                P.add(eng, lambda e: e.tensor_scalar(out=out, in0=in0, scalar1=s1, scalar2=None, op0=op0), r, w)
            else:
                P.add(eng, lambda e: e.tensor_scalar(out=out, in0=in0, scalar1=s1, scalar2=s2, op0=op0, op1=op1), r, w)

        def tt(eng, out, in0, in1, op, r=(), w=()):
            P.add(eng, lambda e: e.tensor_tensor(out=out, in0=in0, in1=in1, op=op), r, w)

        def stt(eng, out, in0, scalar, in1, op0, op1, r=(), w=()):
            P.add(eng, lambda e: e.scalar_tensor_tensor(out=out, in0=in0, scalar=scalar, in1=in1, op0=op0, op1=op1), r, w)

        def rsqrt_chain(v, n, key):
            act(v, v, AF.Sqrt, r=[key], w=[key])
            P.add('dve', lambda e: e.reciprocal(out=v, in_=v), [key], [key])

        o = 0
        cf = V(o, 1536); o += 1536
        cbf = V(o, 320, BF16); o += 320
        nw1 = V(o, 16); o += 16
        nw2 = V(o, 16); o += 16
        cwt = V(o, 80); o += 80
        cbt = V(o, 16); o += 16
        fnwb = V(o, 1024); o += 1024
        bgb = V(o, 32); o += 32
        lamb = V(o, 512); o += 512
        anwb = V(o, 256); o += 256
        mnwb = V(o, 1024); o += 1024
        nlam = V(o, 8); o += 8
        zer = V(o, 8); o += 8
        assert o <= PH
        ident_f = cf[:, 0:128]
        ones_f = cf[:, 128:256]
        tri = [cf[:, 256:384], cf[:, 384:512]]
        msk = [cf[:, 512:1024], cf[:, 1024:1536]]
        ident_b = cbf[:, 0:128]
        dcorr = [cbf[:, 128 + 128 * h:256 + 128 * h] for h in range(4)]

        dma('sp', cf, d_cf[:, :])
        dma('sp', cbf, d_cbf[:, :])
        dma('sp', nw1, d_nw1[:, :])
        dma('sp', nw2, d_nw2[:, :])
        dma('sp', cwt, d_cw[:, :])
        dma('sp', cbt, d_cb[:, :])
        dma('sp', fnwb, d_fnw.partition_broadcast(128))
        dma('sp', bgb, d_bg.partition_broadcast(128))
        dma('sp', lamb, d_lam.partition_broadcast(128))
        dma('sp', anwb, d_anw.partition_broadcast(128))
        dma('sp', mnwb, d_mnw.partition_broadcast(128))
        P.add('dve', lambda e: e.memset(zer, 0.0), [], ['zer'])
        P.barrier()
        lam_init = [0.8 - 0.6 * math.exp(-0.3 * l) for l in range(2)]
        for l in range(2):
            lv = lamb[:, l * 256:(l + 1) * 256].rearrange("p (a b d) -> p a b d", a=2, b=2)
            pr = V(PH, 128, F32, "p (a d) -> p a d", a=2)
            sm2 = V(PH + 128, 2)
            tt('dve', pr, lv[:, :, 0, :], lv[:, :, 1, :], ALU.mult, r=[], w=['pr'])
            P.add('dve', lambda e, sm2=sm2, pr=pr: e.tensor_reduce(out=sm2, in_=pr, axis=AX.X, op=ALU.add), ['pr'], ['sm2'])
            act(sm2, sm2, AF.Exp, r=['sm2'], w=['sm2'])
            stt('dve', nlam[:, l:l + 1], sm2[:, 1:2], -lam_init[l], sm2[:, 0:1], ALU.add, ALU.subtract,
                r=['sm2'], w=['nlam'])
            ts('dve', anwb[:, l * 128:(l + 1) * 128], anwb[:, l * 128:(l + 1) * 128], 1.0 - lam_init[l], None,
               ALU.mult, r=[], w=['anwb'])
        P.barrier()

        def phase1(l, jobs):
            b = PH
            Wb = V(b, 14400, BF16, "p (c n) -> p c n", c=8); b += 14400
            stg = [V(b + i * 1800, 1800, F32, "p (c n) -> p c n", c=8) for i in range(2)]; b += 3600
            xt = [V(b + i * 4096, 4096, F32, "p (s d) -> p s d", s=4) for i in range(2)]; b += 8192
            xn = [V(b + i * 512, 512, BF16) for i in range(2)]; b += 1024
            hT = [V(b + i * 2048, 2048, BF16, "p (c t) -> p c t", c=8) for i in range(2)]; b += 4096
            junk = V(b, 512, BF16); b += 512
            smv = V(b, 16); b += 16
            fo = [V(b + i * 256, 256, BF16) for i in range(4)]; b += 1024
            pret = [V(b + i * 512, 512) for i in range(3)]; b += 1536
            vt = [V(b + i * 258, 258, BF16, "p (h e) -> p h e", h=4) for i in range(2)]; b += 516
            mvt = [V(b + i * 258, 258, BF16, "p (h e) -> p h e", h=4) for i in range(2)]; b += 516
            mot = [V(b + i * 512, 512) for i in range(2)]; b += 1024
            gtt = [V(b + i * 16, 16) for i in range(2)]; b += 32
            assert b <= ACOLS
            wv = w_in[l].rearrange("(c p) n -> p c n", p=128)
            for sl in range(16):
                sk = 'stg%d' % (sl % 2)
                dma('sp', stg[sl % 2], wv[:, :, sl * 225:(sl + 1) * 225], w=[sk])
                for ch in range(8):
                    ts('dve' if ch % 2 == 0 else 'pool', Wb[:, ch, sl * 225:(sl + 1) * 225], stg[sl % 2][:, ch, :],
                       nw1[:, l * 8 + ch:l * 8 + ch + 1], None, ALU.mult, r=[sk], w=[('Wb', sl, ch)])
            for i in range(2):
                P.add('dve', lambda e, a=vt[i][:, :, 128:129]: e.memset(a, 1.0), [], [('vt1', i)])
                P.add('dve', lambda e, a=mvt[i][:, :, 128:129]: e.memset(a, 1.0), [], [('mvt1', i)])
            P.barrier()
            for S, xin, scr_ in jobs:
                cur[0] = scr_
                phase1_job(l, S, xin, Wb, xt, xn, hT, junk, smv, fo, pret, vt, mvt, mot, gtt)

        def phase1_job(l, S, xin, Wb, xt, xn, hT, junk, smv, fo, pret, vt, mvt, mot, gtt):
            NST = S // 512
            for blk in range(8):
                dma('pool', cur[0].PRE[blk, :, 0:2], zer[:, 0:2])
                dma('pool', cur[0].PRE[blk, :, S + 2:S + 4], zer[:, 0:2])
            P.barrier()

            cnt = {'bank': 0, 'ev': 0, 'fo': 0, 'pre': 0}

            def nbank():
                k = 2 + cnt['bank'] % 6
                cnt['bank'] += 1
                return k

            def evac_eng():
                cnt['ev'] += 1
                return 'act' if cnt['ev'] % 2 == 0 else 'dve'

            def normA(s_):
                xb = s_ % 2
                xk = 'xt%d' % xb
                dma('sp', xt[xb], xin[s_ * 512:(s_ + 1) * 512, :].rearrange("(s p) d -> p s d", p=128), w=[xk])
                for sub in range(4):
                    c = (s_ % 2) * 4 + sub
                    act(junk, xt[xb][:, sub, :], AF.Square, r=[xk], w=['junk', ('ss', c)], accum_out=smv[:, c:c + 1])
                    ts('dve', smv[:, c:c + 1], smv[:, c:c + 1], 1.0 / D, EPS, ALU.mult, ALU.add, r=[('ss', c)], w=[('ss', c)])
                    rsqrt_chain(smv[:, c:c + 1], 1, ('ss', c))

            def normB(s_):
                xb = s_ % 2
                xk = 'xt%d' % xb
                for sub in range(4):
                    c = (s_ % 2) * 4 + sub
                    nb = sub % 2
                    act(xn[nb], xt[xb][:, sub, :], AF.Copy, r=[xk, ('ss', c)], w=[('xn', nb)], scale=smv[:, c:c + 1])
                    pb = psb(sub % 2)
                    for ch in range(8):
                        tr(pb[:, ch * 128:(ch + 1) * 128], xn[nb][:, ch * 128:(ch + 1) * 128], ident_b,
                           r=[('xn', nb)], w=[('ps', sub % 2)])
                    cp('dve', hT[xb][:, :, sub * 128:(sub + 1) * 128],
                       pb.rearrange("p (c t) -> p c t", c=8), r=[('ps', sub % 2)], w=[('hT', xb, sub)])

            def mains(s_):
                xb = s_ % 2
                t0 = s_ * 512
                hk = [('hT', xb, sub) for sub in range(4)]
                for cb_ in range(16):
                    if cb_ < 4:
                        col0 = cb_ * 128
                    elif cb_ < 8:
                        col0 = 512 + (cb_ - 4) * 128
                    else:
                        col0 = 1536 + (cb_ - 8) * 128
                    bk = nbank()
                    for ch in range(8):
                        mm(ps[bk][:, :], Wb[:, ch, col0:col0 + 128], hT[xb][:, ch, :], ch == 0, ch == 7,
                           r=hk, w=[('ps', bk)])
                    if cb_ < 8:
                        fi = cnt['fo'] % 4
                        cnt['fo'] += 1
                        eng = evac_eng()
                        if cb_ < 4:
                            cp(eng, fo[fi], ps[bk][:, :], r=[('ps', bk)], w=[('fo', fi)])
                            dma('pool', cur[0].qT[cb_, :, t0:t0 + 512], fo[fi], r=[('fo', fi)])
                        else:
                            if eng == 'act':
                                act(fo[fi], ps[bk][:, :], AF.Copy, r=[('ps', bk)], w=[('fo', fi)], scale=0.125)
                            else:
                                ts('dve', fo[fi], ps[bk][:, :], 0.125, None, ALU.mult, r=[('ps', bk)], w=[('fo', fi)])
                            dma('pool', cur[0].kT[cb_ - 4, :, t0:t0 + 512], fo[fi], r=[('fo', fi)])
                    else:
                        pi = cnt['pre'] % 3
                        cnt['pre'] += 1
                        cp(evac_eng(), pret[pi], ps[bk][:, :], r=[('ps', bk)], w=[('pre', pi)])
                        dma('pool', cur[0].PRE[cb_ - 8, :, 2 + t0:2 + t0 + 512], pret[pi], r=[('pre', pi)])
                for sub in range(4):
                    tk = t0 + sub * 128
                    i2 = sub % 2
                    for which, col0 in (('av', 1024), ('mv', 2560), ('mo', 3072)):
                        bk = nbank()
                        for ch in range(8):
                            mm(ps[bk][:, :], hT[xb][:, ch, sub * 128:(sub + 1) * 128], Wb[:, ch, col0:col0 + 512],
                               ch == 0, ch == 7, r=[hk[sub]], w=[('ps', bk)])
                        src = ps[bk][:, :]
                        if which == 'av':
                            cp(evac_eng(), vt[i2][:, :, 0:128], src.rearrange("p (h e) -> p h e", h=4),
                               r=[('ps', bk), ('vt1', i2)], w=[('vt', i2)])
                            dma('pool', cur[0].Vx[tk:tk + 128, :, :], vt[i2], r=[('vt', i2)])
                        elif which == 'mv':
                            cp(evac_eng(), mvt[i2][:, :, 0:128], src.rearrange("p (h e) -> p h e", h=4),
                               r=[('ps', bk), ('mvt1', i2)], w=[('mvt', i2)])
                            dma('pool', cur[0].MVx[tk:tk + 128, :, :], mvt[i2], r=[('mvt', i2)])
                        else:
                            cp(evac_eng(), mot[i2], src, r=[('ps', bk)], w=[('mot', i2)])
                            dma('pool', cur[0].MO[tk:tk + 128, :], mot[i2], r=[('mot', i2)])
                    bk = nbank()
                    for ch in range(8):
                        mm(ps[bk][:, 0:16], hT[xb][:, ch, sub * 128:(sub + 1) * 128], Wb[:, ch, 3584:3600],
                           ch == 0, ch == 7, r=[hk[sub]], w=[('ps', bk)])
                    tt('dve', gtt[i2], ps[bk][:, 0:16], bgb[:, l * 16:(l + 1) * 16], ALU.add,
                       r=[('ps', bk)], w=[('gtt', i2)])
                    dma('pool', cur[0].G[tk:tk + 128, :], gtt[i2], r=[('gtt', i2)])

            normA(0)
            normB(0)
            for s_ in range(NST):
                if s_ + 1 < NST:
                    normA(s_ + 1)
                mains(s_)
                if s_ + 1 < NST:
                    normB(s_ + 1)
            P.barrier()

        def phase_conv(l, S):
            b = PH
            dg = V(b, 2560, BF16, "p (a k c) -> p a k c", a=8, k=5); b += 2560
            ptile = [V(b + i * 520, 516) for i in range(3)]; b += 1560
            xb = [V(b + i * 260, 258, BF16) for i in range(3)]; b += 780
            so = [V(b + i * 512, 512) for i in range(2)]; b += 1024
            ob = [V(b + i * 256, 256, BF16) for i in range(3)]; b += 768
            for blk in range(8):
                for k in range(5):
                    c0 = (l * 8 + blk) * 5 + k
                    ts('dve' if (blk * 5 + k) % 2 == 0 else 'pool', dg[:, blk, k, :], ident_f, cwt[:, c0:c0 + 1], None,
                       ALU.mult, r=[], w=[('dg', blk, k)])
            P.barrier()
            i = 0
            for s_ in range(S // 512):
                t0 = s_ * 512
                for blk in range(8):
                    pi, ai, bk = i % 3, i % 2, i % 8
                    i += 1
                    dma('sp', ptile[pi], cur[0].PRE[blk, :, t0:t0 + 516], w=[('pt', pi)])
                    cp('dve' if i % 2 == 0 else 'pool', xb[pi][:, 0:516], ptile[pi], r=[('pt', pi)], w=[('xb', pi)])
                    for k in range(5):
                        mm(ps[bk][:, :], dg[:, blk, k, :], xb[pi][:, k:k + 512], k == 0, k == 4,
                           r=[('xb', pi)], w=[('ps', bk)])
                    bias = cbt[:, l * 8 + blk:l * 8 + blk + 1]
                    if blk < 4:
                        act(ob[pi], ps[bk][:, :], AF.Silu, r=[('ps', bk)], w=[('ob', pi)], bias=bias)
                        dma('pool', cur[0].mqT[blk * 128:(blk + 1) * 128, t0:t0 + 512], ob[pi], r=[('ob', pi)])
                    else:
                        act(so[ai], ps[bk][:, :], AF.Silu, r=[('ps', bk)], w=[('so', ai)], bias=bias)
                        ts('pool' if i % 2 == 0 else 'dve', ob[pi], so[ai], 128.0 ** -0.5, None, ALU.mult,
                           r=[('so', ai)], w=[('ob', pi)])
                        dma('pool', cur[0].mkT[(blk - 4) * 128:(blk - 3) * 128, t0:t0 + 512], ob[pi], r=[('ob', pi)])
            P.barrier()

        def key_tiles(h, qt, NKT):
            if att_full:
                return list(range(NKT))
            dmin = 40.0 / SLOPES[h]
            lo = max(0, int(math.floor((qt * 512 - dmin) / 128.0)))
            hi = min(NKT - 1, int(math.floor((qt * 512 + 511 + dmin) / 128.0)))
            return list(range(lo, hi + 1))

        def phase_att(l, S):
            NKT = S // 128
            b = PH
            kTs = V(b, S, BF16, "p (m t) -> p m t", m=2); b += S
            Vs = V(b, (NKT * 129 + 1) // 2, BF16)[:, 0:NKT * 129].rearrange("p (k e) -> p k e", k=NKT); b += (NKT * 129 + 1) // 2
            qx = [V(b + i * 1024, 1024, BF16, "p (s m t) -> p s m t", s=2, m=2) for i in range(2)]; b += 2048
            pT2 = [V(b + i * 512, 512, BF16, "p (m t) -> p m t", m=2) for i in range(3)]
            pT = [[pT2[i][:, m, :] for m in range(2)] for i in range(3)]; b += 1536
            ov = V(b, 1032, F32, "p (a e) -> p a e", a=8); b += 1032
            av = [V(b + i * 128, 128) for i in range(2)]; b += 256
            jk = V(b, 128); b += 128
            obf = [V(b + i * 64, 64, BF16) for i in range(2)]; b += 128
            ot = [V(b + i * 256, 256, BF16) for i in range(2)]; b += 512
            sm = V(b, 32); b += 32
            assert b <= ACOLS
            accreg = [(4 + (a // 3), (a % 3) * 129) for a in range(8)]
            qi = 0
            pend = [None]
            for h in range(4):
                kst = min(2048, S)
                kkeys, vkeys = [], []
                for c0 in range(0, S, kst):
                    kkeys.append(('kT', c0))
                    dma('sp', kTs[0:64, :, c0:c0 + kst],
                        cur[0].kT[h, :, c0:c0 + kst].rearrange("(m d) t -> d m t", m=2), w=[kkeys[-1]])
                for m in range(2):
                    kkeys.append(('kTp', m))
                    dma('sp', kTs[64:68, m, :], d_kpos[h, :, 0:S], w=[kkeys[-1]])
                vst = min(16, NKT)
                for k0 in range(0, NKT, vst):
                    vkeys.append(('Vs', k0))
                    dma('sp', Vs[:, k0:k0 + vst, :],
                        cur[0].Vx[k0 * 128:(k0 + vst) * 128, h, :].rearrange("(k p) e -> p k e", p=128), w=[vkeys[-1]])
                for qt in range(S // 512):
                    t0 = qt * 512
                    qb = qi % 2
                    qi += 1
                    qk = ('qx', qb)
                    for sg in range(2):
                        dma('sp', qx[qb][0:64, sg, :, :],
                            cur[0].qT[h, :, t0:t0 + 512].rearrange("(m d) t -> d m t", m=2), w=[qk])
                    for m in range(2):
                        dma('sp', qx[qb][64:68, :, m, :],
                            d_posq[:, :, t0:t0 + 512].rearrange("s r t -> r s t"), w=[qk])
                    kts = key_tiles(h, qt, NKT)
                    n = len(kts)

                    def stageA(i):
                        kt = kts[i]
                        for m in range(2):
                            bk = (i % 2) * 2 + m
                            kl = kTs[0:68, m, kt * 128:(kt + 1) * 128]
                            rr = kkeys + [qk]
                            ww = [('ps', bk)]
                            if kt < 4 * qt:
                                mm(ps[bk][:, :], kl, qx[qb][0:68, 0, m, :], True, True, r=rr, w=ww)
                            elif kt > 4 * qt + 3:
                                mm(ps[bk][:, :], kl, qx[qb][0:68, 1, m, :], True, True, r=rr, w=ww)
                            else:
                                d = kt - 4 * qt
                                if d > 0:
                                    mm(ps[bk][:, 0:128 * d], kl, qx[qb][0:68, 1, m, 0:128 * d], True, True, r=rr, w=ww)
                                mm(ps[bk][:, 128 * d:128 * (d + 1)], kl, qx[qb][0:68, 0, m, 128 * d:128 * (d + 1)],
                                   True, False, r=rr, w=ww)
                                mm(ps[bk][:, 128 * d:128 * (d + 1)], ident_b, dcorr[h], False, True, r=rr, w=ww)
                                if d < 3:
                                    mm(ps[bk][:, 128 * (d + 1):512], kl, qx[qb][0:68, 0, m, 128 * (d + 1):512],
                                       True, True, r=rr, w=ww)

                    def stageB(i):
                        b0 = (i % 2) * 2
                        act(pT2[i % 3], psq[:, b0:b0 + 2, :], AF.Exp, r=[('ps', b0), ('ps', b0 + 1)],
                            w=[('pT', i % 3, 0), ('pT', i % 3, 1)])

                    def stageC(i):
                        kt = kts[i]
                        for m in range(2):
                            for sub in range(4):
                                bk, c0 = accreg[m * 4 + sub]
                                a_ = m * 4 + sub
                                mm(ps[bk][:, c0:c0 + 129], pT[i % 3][m][:, sub * 128:(sub + 1) * 128], Vs[:, kt, :],
                                   i == 0 and a_ % 3 == 0, i == n - 1 and (a_ % 3 == 2 or a_ == 7),
                                   r=[('pT', i % 3, m)] + vkeys, w=[('acc', bk)])

                    stageA(0)
                    if n > 1:
                        stageA(1)
                    for i in range(n):
                        stageB(i)
                        stageC(i)
                        if i + 2 < n:
                            stageA(i + 2)
                        if i == 1 and pend[0] is not None:
                            pend[0]()
                            pend[0] = None
                    for bk in range(4, 7):
                        na = 3 if bk < 6 else 2
                        cp('dve', ov[:, (bk - 4) * 3:(bk - 4) * 3 + na, :],
                           ps[bk][:, 0:na * 129].rearrange("p (a e) -> p a e", a=na),
                           r=[('acc', bk)], w=[('ov', bk)])
                    ovk = [('ov', 4), ('ov', 5), ('ov', 6)]
                    rinv = sm[:, 0:8]
                    P.add('dve', lambda e, rinv=rinv, ov=ov: e.reciprocal(out=rinv.unsqueeze(2), in_=ov[:, :, 128:129]),
                          ovk, ['rinv'])
                    ts('dve', sm[:, 8:12], sm[:, 4:8], nlam[:, l:l + 1], None, ALU.mult, r=['rinv'], w=['rl'])
                    oi = qi % 2

                    def part2(h=h, t0=t0, oi=oi):
                        pb = psb(7)
                        for sub in range(4):
                            tr(pb[:, sub * 128:(sub + 1) * 128], obf_all[oi][:, sub * 128:(sub + 1) * 128], ident_b,
                               r=[('obf', oi)], w=[('ps', 7)])
                        cp('act', ot[oi], pb[:, 0:512], r=[('ps', 7)], w=[('ot', oi)])
                        dma('pool', cur[0].mixT[h * 128:(h + 1) * 128, t0:t0 + 512], ot[oi], r=[('ot', oi)])

                    for sub in range(4):
                        ai = sub % 2
                        ts('dve', av[ai], ov[:, sub, 0:128], sm[:, sub:sub + 1], None, ALU.mult,
                           r=ovk + ['rinv'], w=[('av', ai)])
                        stt('dve', av[ai], ov[:, 4 + sub, 0:128], sm[:, 8 + sub:9 + sub], av[ai], ALU.mult, ALU.add,
                            r=ovk + ['rl', ('av', ai)], w=[('av', ai)])
                        sk = ('ssq', sub)
                        act(jk, av[ai], AF.Square, r=[('av', ai)], w=['jk', sk], accum_out=sm[:, 16 + sub:17 + sub])
                        ts('dve', sm[:, 16 + sub:17 + sub], sm[:, 16 + sub:17 + sub], 1.0 / 128, EPS, ALU.mult, ALU.add,
                           r=[sk], w=[sk])
                        rsqrt_chain(sm[:, 16 + sub:17 + sub], 1, sk)
                        stt('dve', obf_all[oi][:, sub * 128:(sub + 1) * 128], av[ai], sm[:, 16 + sub:17 + sub],
                            anwb[:, l * 128:(l + 1) * 128], ALU.mult, ALU.mult, r=[('av', ai), sk], w=[('obf', oi)])
                    pend[0] = part2
            if pend[0] is not None:
                pend[0]()
                pend[0] = None
            P.barrier()

        obf_all = [V(ACOLS - 512 + i * 256, 256, BF16) for i in range(2)]

        def phase_mlstm(l, S):
            NC = S // 128
            b = PH

            def A(n, dt=F32, pat=None, **kw):
                nonlocal b
                v = V(b, n, dt, pat, **kw)
                b += n
                return v

            def A2(n, dt=F32, pat=None, **kw):
                return [A(n, dt, pat, **kw) for _ in range(2)]
            qc = A2(256, BF16, "p (h t) -> p h t", h=4)
            kc = A2(256, BF16, "p (h t) -> p h t", h=4)
            vx = A2(258, BF16, "p (h e) -> p h e", h=4)
            gt = A2(16)
            mo = A2(512)
            hfl = A2(512)
            g4 = A2(32)
            pe16 = A2(16)
            ex = A2(16)
            vs1 = A2(258, BF16, "p (h e) -> p h e", h=4)
            vs2 = A2(258, BF16, "p (h e) -> p h e", h=4)
            kk = A2(256, BF16, "p (h d) -> p h d", h=4)
            smk = A2(256, BF16, "p (h j) -> p h j", h=4)
            nd = A(516, F32, "p (h e) -> p h e", h=4)
            dd = A(16)
            hd = A2(512, F32, "p (h e) -> p h e", h=4)
            Cst = A(516, F32, "p (h e) -> p h e", h=4)
            Cb = A(258, BF16, "p (h e) -> p h e", h=4)
            hs = A(512, F32, "p (h e) -> p h e", h=4)
            sq = A(512, F32, "p (h e) -> p h e", h=4)
            og = A(512, F32, "p (h e) -> p h e", h=4)
            memb = A(256, BF16, "p (h e) -> p h e", h=4)
            mst = A2(1024, BF16, "p (h t) -> p h t", h=4)
            assert b <= ACOLS - 512
            for d in range(2):
                P.add('dve', lambda e: e.memset(Cst, 0.0), [], ['Cst'])
                P.add('pool', lambda e: e.memset(Cb, 0.0), [], ['Cb'])
                order = list(range(NC)) if d == 0 else list(range(NC - 1, -1, -1))

                def stage1(ii, d=d, order=order):
                    c = order[ii]
                    t0 = c * 128
                    bi = ii % 2
                    dma('sp', qc[bi], cur[0].mqT[:, t0:t0 + 128].rearrange("(h d) t -> d h t", h=4), w=[('qc', bi)])
                    dma('sp', kc[bi], cur[0].mkT[:, t0:t0 + 128].rearrange("(h d) t -> d h t", h=4), w=[('kc', bi)])
                    dma('sp', vx[bi], cur[0].MVx[t0:t0 + 128, :, :], w=[('vx', bi)])
                    dma('sp', gt[bi], cur[0].G[t0:t0 + 128, :], w=[('gt', bi)])
                    if d == 1:
                        dma('sp', mo[bi], cur[0].MO[t0:t0 + 128, :], w=[('mo', bi)])
                        dma('sp', hfl[bi], cur[0].HF[t0:t0 + 128, :], w=[('hfl', bi)])
                    fg = gt[bi][:, 8 + 4 * d:12 + 4 * d]
                    ig = gt[bi][:, 4 * d:4 * d + 4]
                    g_ = g4[bi]
                    ab, t1, l1, lf = g_[:, 0:4], g_[:, 4:8], g_[:, 8:12], g_[:, 12:16]
                    stt('dve', ab, fg, -1.0, fg, ALU.mult, ALU.max, r=[('gt', bi)], w=[('ab', bi)])
                    act(t1, ab, AF.Exp, r=[('ab', bi)], w=[('t1', bi)], scale=-1.0)
                    act(l1, t1, AF.Ln, r=[('t1', bi)], w=[('l1', bi)], bias=1.0)
                    stt('dve', lf, fg, 0.0, l1, ALU.min, ALU.subtract, r=[('gt', bi), ('l1', bi)], w=[('lf', bi)])
                    mm(ps[0][:, 0:4], tri[d], lf, True, True, r=[('lf', bi)], w=[('ps', 0)])
                    mm(ps[0][:, 4:8], ones_f, lf, True, True, r=[('lf', bi)], w=[('ps', 0)])
                    p16 = pe16[bi]
                    pk = ('pe16', bi)
                    tt('dve', p16[:, 0:4], ig, ps[0][:, 0:4], ALU.subtract, r=[('gt', bi), ('ps', 0)], w=[pk])
                    cp('dve', p16[:, 4:8], ps[0][:, 0:4], r=[('ps', 0)], w=[pk])
                    tt('dve', p16[:, 8:12], p16[:, 0:4], ps[0][:, 4:8], ALU.add, r=[pk, ('ps', 0)], w=[pk])
                    cp('dve', p16[:, 12:16], ps[0][:, 4:8], r=[('ps', 0)], w=[pk])
                    ek = ('ex', bi)
                    act(ex[bi], p16, AF.Exp, r=[pk], w=[ek])
                    tt('dve', vs1[bi], vx[bi], ex[bi][:, 0:4].unsqueeze(2).to_broadcast([128, 4, 129]), ALU.mult,
                       r=[('vx', bi), ek], w=[('vs1', bi)])
                    tt('pool', vs2[bi], vx[bi], ex[bi][:, 8:12].unsqueeze(2).to_broadcast([128, 4, 129]), ALU.mult,
                       r=[('vx', bi), ek], w=[('vs2', bi)])
                    pkk = psb(1)
                    for h in range(4):
                        tr(pkk[:, h * 128:(h + 1) * 128], kc[bi][:, h, :], ident_b, r=[('kc', bi)], w=[('ps', 1)])
                    cp('act', kk[bi], pkk[:, 0:512].rearrange("p (h d) -> p h d", h=4), r=[('ps', 1)], w=[('kk', bi)])
                    for h in range(4):
                        mm(ps[2][:, h * 128:(h + 1) * 128], kc[bi][:, h, :], qc[bi][:, h, :], True, True,
                           r=[('kc', bi), ('qc', bi)], w=[('ps', 2)])
                    tt('dve', smk[bi], ps[2][:, :].rearrange("p (h j) -> p h j", h=4),
                       msk[d].rearrange("p (h j) -> p h j", h=4), ALU.mult, r=[('ps', 2)], w=[('smk', bi)])

                def stage2(ii, d=d, order=order):
                    c = order[ii]
                    t0 = c * 128
                    bi = ii % 2
                    ek = ('ex', bi)
                    for h in range(4):
                        bk = 5 + h // 2
                        c0 = (h % 2) * 129
                        mm(ps[bk][:, c0:c0 + 129], kk[bi][:, h, :], vs2[bi][:, h, :], True, True,
                           r=[('kk', bi), ('vs2', bi)], w=[('ps', bk)])
                    for h in range(4):
                        bk = 3 + h // 2
                        c0 = (h % 2) * 129
                        mm(ps[bk][:, c0:c0 + 129], qc[bi][:, h, :], Cb[:, h, :], True, False,
                           r=[('qc', bi), 'Cb'], w=[('ps', bk)])
                        mm(ps[bk][:, c0:c0 + 129], smk[bi][:, h, :], vs1[bi][:, h, :], False, True,
                           r=[('smk', bi), ('vs1', bi)], w=[('ps', bk)])
                    for h in range(4):
                        bk = 5 + h // 2
                        c0 = (h % 2) * 129
                        stt('dve', Cst[:, h, :], Cst[:, h, :], ex[bi][:, 12 + h:13 + h], ps[bk][:, c0:c0 + 129],
                            ALU.mult, ALU.add, r=['Cst', ek, ('ps', bk)], w=['Cst'])
                    cp('pool', Cb, Cst, r=['Cst'], w=['Cb'])
                    for half in range(2):
                        tt('dve', nd[:, 2 * half:2 * half + 2, :],
                           ps[3 + half][:, 0:258].rearrange("p (h e) -> p h e", h=2),
                           ex[bi][:, 4 + 2 * half:6 + 2 * half].unsqueeze(2).to_broadcast([128, 2, 129]), ALU.mult,
                           r=[('ps', 3 + half), ek], w=[('nd', half)])
                    ndk = [('nd', 0), ('nd', 1)]
                    stt('dve', dd[:, 0:4].unsqueeze(2), nd[:, :, 128:129], -1.0, nd[:, :, 128:129], ALU.mult, ALU.max,
                        r=ndk, w=['dd'])
                    ts('dve', dd[:, 0:4], dd[:, 0:4], 1.0, None, ALU.max, r=['dd'], w=['dd'])
                    P.add('dve', lambda e, dd=dd: e.reciprocal(out=dd[:, 4:8], in_=dd[:, 0:4]), ['dd'], ['rd'])
                    hi = ii % 2
                    tt('dve', hd[hi], nd[:, :, 0:128], dd[:, 4:8].unsqueeze(2).to_broadcast([128, 4, 128]), ALU.mult,
                       r=ndk + ['rd'], w=[('hd', hi)])
                    if d == 0:
                        dma('pool', cur[0].HF[t0:t0 + 128, :], hd[hi].rearrange("p h e -> p (h e)"), r=[('hd', hi)])
                    else:
                        tt('dve', hs, hd[hi], hfl[bi].rearrange("p (h e) -> p h e", h=4), ALU.add,
                           r=[('hd', hi), ('hfl', bi)], w=['hs'])
                        tt('pool', sq, hs, hs, ALU.mult, r=['hs'], w=['sq'])
                        P.add('dve', lambda e, dd=dd, sq=sq: e.tensor_reduce(out=dd[:, 8:12], in_=sq, axis=AX.X, op=ALU.add),
                              ['sq'], ['ss4'])
                        ts('dve', dd[:, 8:12], dd[:, 8:12], 1.0 / 128, EPS, ALU.mult, ALU.add, r=['ss4'], w=['ss4'])
                        rsqrt_chain(dd[:, 8:12], 4, 'ss4')
                        tt('dve', hs, hs, dd[:, 8:12].unsqueeze(2).to_broadcast([128, 4, 128]), ALU.mult,
                           r=['hs', 'ss4'], w=['hs'])
                        tt('pool', hs, hs, mnwb[:, l * 512:(l + 1) * 512].rearrange("p (h e) -> p h e", h=4), ALU.mult,
                           r=['hs'], w=['hs'])
                        act(og, mo[bi].rearrange("p (h e) -> p h e", h=4), AF.Sigmoid, r=[('mo', bi)], w=['og'])
                        tt('dve', memb, og, hs, ALU.mult, r=['og', 'hs'], w=['memb'])
                        pm = psb(7)
                        for h in range(4):
                            tr(pm[:, h * 128:(h + 1) * 128], memb[:, h, :], ident_b, r=['memb'], w=[('ps', 7)])
                        grp = c // 4
                        gi = grp % 2
                        cp('act', mst[gi][:, :, (c % 4) * 128:(c % 4 + 1) * 128],
                           pm[:, 0:512].rearrange("p (h t) -> p h t", h=4), r=[('ps', 7)], w=[('mst', gi)])
                        if c % 4 == 0:
                            dma('pool', cur[0].mixT[512:1024, grp * 512:(grp + 1) * 512].rearrange("(h e) t -> e h t", h=4),
                                mst[gi], r=[('mst', gi)])

                stage1(0)
                for ii in range(NC):
                    if ii + 1 < NC:
                        stage1(ii + 1)
                    stage2(ii)
                P.barrier()

        def phase3(l, jobs):
            b = PH
            Wo = V(b, 4096, BF16, "p (c n) -> p c n", c=8); b += 4096
            W1 = V(b, 22528, BF16, "p (c n) -> p c n", c=8); b += 22528
            W2 = V(b, 11264, BF16, "p (c n) -> p c n", c=22); b += 11264
            xt = V(b, 2048, F32, "p (s d) -> p s d", s=2); b += 2048
            sb0 = b
            mx = V(b, 1024, BF16, "p (c t) -> p c t", c=8); b += 1024
            hT = V(b, 1024, BF16, "p (c t) -> p c t", c=8); b += 1024
            aT = V(b, 2816, BF16, "p (c t) -> p c t", c=22); b += 2816
            xn = V(b, 512, BF16); b += 512
            junk = V(b, 512, BF16); b += 512
            sg = [V(b + i * 256, 256) for i in range(2)]; b += 512
            smv = V(b, 16); b += 16
            assert b <= ACOLS - 512, b
            stg = [V(sb0 + i * 2048, 2048, F32, "p (c n) -> p c n", c=8) for i in range(2)]
            stg2 = [V(sb0 + i * 2048, 2048, F32, "p (c n) -> p c n", c=2) for i in range(2)]
            si = 0
            wv = w_out[l].rearrange("(c p) n -> p c n", p=128)
            for sl in range(4):
                sk = 'stg%d' % (si % 2)
                dma('sp', stg[si % 2], wv[:, :, sl * 256:(sl + 1) * 256], w=[sk])
                for ch in range(8):
                    cp('dve' if ch % 2 == 0 else 'pool', Wo[:, ch, sl * 256:(sl + 1) * 256], stg[si % 2][:, ch, :],
                       r=[sk], w=[('Wo', sl, ch)])
                si += 1
            wv = w_f1[l].rearrange("(c p) n -> p c n", p=128)
            for sl in range(22):
                sk = 'stg%d' % (si % 2)
                dma('sp', stg[si % 2], wv[:, :, sl * 256:(sl + 1) * 256], w=[sk])
                for ch in range(8):
                    ts('dve' if ch % 2 == 0 else 'pool', W1[:, ch, sl * 256:(sl + 1) * 256], stg[si % 2][:, ch, :],
                       nw2[:, l * 8 + ch:l * 8 + ch + 1], None, ALU.mult, r=[sk], w=[('W1', sl, ch)])
                si += 1
            wv = w_f2[l].rearrange("(c p) n -> p c n", p=128)
            for sl in range(11):
                sk = 'stg%d' % (si % 2)
                dma('sp', stg2[si % 2], wv[:, 2 * sl:2 * sl + 2, :], w=[sk])
                for ch in range(2):
                    cp('dve' if ch % 2 == 0 else 'pool', W2[:, 2 * sl + ch, :], stg2[si % 2][:, ch, :],
                       r=[sk], w=[('W2', sl, ch)])
                si += 1
            P.barrier()
            for S, xin, xout, final, scr_ in jobs:
                cur[0] = scr_
                phase3_job(l, S, xin, xout, final, Wo, W1, W2, xt, mx, hT, aT, xn, junk, sg, smv)

        def phase3_job(l, S, xin, xout, final, Wo, W1, W2, xt, mx, hT, aT, xn, junk, sg, smv):
            NT = S // 256
            cnt = {'bank': 0}

            def nbank():
                k = 1 + cnt['bank'] % 7
                cnt['bank'] += 1
                return k

            for it in range(NT):
                t0 = it * 256
                dma('sp', xt, xin[t0:t0 + 256, :].rearrange("(s p) d -> p s d", p=128), w=['xt'])
                dma('sp', mx, cur[0].mixT[:, t0:t0 + 256].rearrange("(c p) t -> p c t", p=128), w=['mx'])
                for sub in range(2):
                    for n in range(2):
                        bk = nbank()
                        for ch in range(8):
                            mm(ps[bk][:, :], mx[:, ch, sub * 128:(sub + 1) * 128], Wo[:, ch, n * 512:(n + 1) * 512],
                               ch == 0, ch == 7, r=['mx'], w=[('ps', bk)])
                        tt('dve', xt[:, sub, n * 512:(n + 1) * 512], xt[:, sub, n * 512:(n + 1) * 512], ps[bk][:, :],
                           ALU.add, r=[('ps', bk), 'xt'], w=['xt'])
                for sub in range(2):
                    act(junk, xt[:, sub, :], AF.Square, r=['xt'], w=['junk', ('ss', sub)], accum_out=smv[:, sub:sub + 1])
                    ts('dve', smv[:, sub:sub + 1], smv[:, sub:sub + 1], 1.0 / D, EPS, ALU.mult, ALU.add,
                       r=[('ss', sub)], w=[('ss', sub)])
                    rsqrt_chain(smv[:, sub:sub + 1], 1, ('ss', sub))
                    act(xn, xt[:, sub, :], AF.Copy, r=['xt', ('ss', sub)], w=['xn'], scale=smv[:, sub:sub + 1])
                    pb = psb(0)
                    for ch in range(8):
                        tr(pb[:, ch * 128:(ch + 1) * 128], xn[:, ch * 128:(ch + 1) * 128], ident_b, r=['xn'], w=[('ps', 0)])
                    cp('dve', hT[:, :, sub * 128:(sub + 1) * 128], pb.rearrange("p (c t) -> p c t", c=8),
                       r=[('ps', 0)], w=[('hT', sub)])
                hk = [('hT', 0), ('hT', 1)]
                for j in range(22):
                    ba = nbank()
                    for ch in range(8):
                        mm(ps[ba][:, 0:256], W1[:, ch, j * 128:(j + 1) * 128], hT[:, ch, :], ch == 0, ch == 7,
                           r=hk, w=[('ps', ba)])
                    bb = nbank()
                    for ch in range(8):
                        mm(ps[bb][:, 0:256], W1[:, ch, FF + j * 128:FF + (j + 1) * 128], hT[:, ch, :], ch == 0, ch == 7,
                           r=hk, w=[('ps', bb)])
                    act(sg[j % 2], ps[ba][:, 0:256], AF.Silu, r=[('ps', ba)], w=[('sg', j % 2)])
                    tt('dve', aT[:, j, :], sg[j % 2], ps[bb][:, 0:256], ALU.mult, r=[('sg', j % 2), ('ps', bb)],
                       w=[('aT', j)])
                ak = [('aT', j) for j in range(22)]
                for sub in range(2):
                    for n in range(2):
                        bk = nbank()
                        for j in range(22):
                            mm(ps[bk][:, :], aT[:, j, sub * 128:(sub + 1) * 128], W2[:, j, n * 512:(n + 1) * 512],
                               j == 0, j == 21, r=ak, w=[('ps', bk)])
                        tt('dve', xt[:, sub, n * 512:(n + 1) * 512], xt[:, sub, n * 512:(n + 1) * 512], ps[bk][:, :],
                           ALU.add, r=[('ps', bk), 'xt'], w=['xt'])
                if final:
                    for sub in range(2):
                        c = 4 + sub
                        act(junk, xt[:, sub, :], AF.Square, r=['xt'], w=['junk', ('ss', c)], accum_out=smv[:, c:c + 1])
                        ts('dve', smv[:, c:c + 1], smv[:, c:c + 1], 1.0 / D, EPS, ALU.mult, ALU.add,
                           r=[('ss', c)], w=[('ss', c)])
                        rsqrt_chain(smv[:, c:c + 1], 1, ('ss', c))
                        stt('dve', xt[:, sub, :], xt[:, sub, :], smv[:, c:c + 1], fnwb, ALU.mult, ALU.mult,
                            r=['xt', ('ss', c)], w=['xt'])
                dma('pool', xout[t0:t0 + 256, :].rearrange("(s p) d -> p s d", p=128), xt, r=['xt'])
            P.barrier()

        for l in range(nlayers):
            xin_ = [xs[i] if l == 0 else SCS[i].X1[0:S, :] for i, S in enumerate(seq_lens)]
            xout_ = [ys[i] if l == 1 else SCS[i].X1[0:S, :] for i, S in enumerate(seq_lens)]
            phase1(l, [(S, xin_[i], SCS[i]) for i, S in enumerate(seq_lens)])
            for i, S in enumerate(seq_lens):
                cur[0] = SCS[i]
                phase_conv(l, S)
                phase_att(l, S)
                phase_mlstm(l, S)
            phase3(l, [(S, xin_[i], xout_[i], l == 1, SCS[i]) for i, S in enumerate(seq_lens)])
        P.emit(block)
    return nc


def host_consts(Smax):
    cf = np.zeros((128, 1536), np.float32)
    a = np.arange(128)
    cf[:, 0:128] = np.eye(128)
    cf[:, 128:256] = 1.0
    le = (a[:, None] <= a[None, :]).astype(np.float32)
    ge = (a[:, None] >= a[None, :]).astype(np.float32)
    cf[:, 256:384] = le
    cf[:, 384:512] = ge
    cf[:, 512:1024] = np.tile(le, (1, 4))
    cf[:, 1024:1536] = np.tile(ge, (1, 4))
    cb = np.zeros((128, 640), np.float32)
    cb[:, 0:128] = np.eye(128)
    for h in range(4):
        cb[:, 128 + 128 * h:256 + 128 * h] = -2.0 * SLOPES[h] * np.maximum(a[:, None] - a[None, :], 0)
    t = np.arange(Smax)
    ta, tb = (t // 128).astype(np.float32), (t % 128).astype(np.float32)
    posq = np.zeros((2, 4, Smax), np.float32)
    for s, sg in enumerate((1.0, -1.0)):
        posq[s, 0] = sg * ta
        posq[s, 1] = sg * tb
        posq[s, 2] = sg
        posq[s, 3] = sg
    kpos = np.zeros((4, 4, Smax), np.float32)
    for h in range(4):
        kpos[h, 0] = -SLOPES[h] * 128
        kpos[h, 1] = -SLOPES[h]
        kpos[h, 2] = SLOPES[h] * 128 * ta
        kpos[h, 3] = SLOPES[h] * tb
    return {"c_f32": cf, "c_bf16": cb.astype(NBF), "posq": posq.astype(NBF), "kpos": kpos.astype(NBF)}


def host_weights(norm1_w, w_in, b_gate, conv_w, conv_b, lam, att_norm_w, mlstm_norm_w, w_out, norm2_w,
                 w_ffn_in, w_ffn_out, final_norm_w):
    f = lambda a: np.ascontiguousarray(np.asarray(a, np.float32))
    m = {}
    m["w_in"] = f(w_in)
    m["w_out"] = f(w_out)
    m["w_ffn_in"] = f(w_ffn_in)
    m["w_ffn_out"] = f(w_ffn_out)
    m["norm1_w"] = f(np.asarray(norm1_w).reshape(2, 8, 128).transpose(2, 0, 1).reshape(128, 16))
    m["norm2_w"] = f(np.asarray(norm2_w).reshape(2, 8, 128).transpose(2, 0, 1).reshape(128, 16))
    m["final_norm_w"] = f(np.asarray(final_norm_w).reshape(1, D))
    m["b_gate"] = f(np.asarray(b_gate).reshape(1, 32))
    m["conv_w"] = f(np.asarray(conv_w).reshape(2, 5, 8, 128).transpose(3, 0, 2, 1).reshape(128, 80))
    m["conv_b"] = f(np.asarray(conv_b).reshape(2, 8, 128).transpose(2, 0, 1).reshape(128, 16))
    m["lam"] = f(np.asarray(lam).reshape(1, 512))
    m["att_norm_w"] = f(np.asarray(att_norm_w).reshape(1, 256))
    m["mlstm_norm_w"] = f(np.asarray(mlstm_norm_w).reshape(1, 1024))
    return m


_CACHE = {}


def kernel(x_prompt, x_sample, norm1_w, w_in, b_gate, conv_w, conv_b, lam, att_norm_w, mlstm_norm_w,
           w_out, norm2_w, w_ffn_in, w_ffn_out, final_norm_w):
    x_prompt = np.asarray(x_prompt, np.float32)
    x_sample = np.asarray(x_sample, np.float32)
    B, S, _ = x_prompt.shape
    DB, DS, _ = x_sample.shape
    n = 8
    key = (S, DS)
    if key not in _CACHE:
        _CACHE[key] = build([S, DS])
    nc = _CACHE[key]
    base = host_weights(norm1_w, w_in, b_gate, conv_w, conv_b, lam, att_norm_w, mlstm_norm_w, w_out, norm2_w,
                        w_ffn_in, w_ffn_out, final_norm_w)
    base.update(host_consts(max(S, DS)))
    in_maps = []
    for c in range(n):
        m = dict(base)
        m["x0"] = np.ascontiguousarray(x_prompt[c])
        m["x1"] = np.ascontiguousarray(x_sample[c // 4])
        in_maps.append(m)
    res = run_bass_kernel_spmd(nc, in_maps, core_ids=list(range(n)))
    y_prompt = np.stack([np.asarray(res.results[c]["y0"], np.float32) for c in range(n)], axis=0)
    q = DS // 4
    y_sample = np.zeros((DB, DS, D), np.float32)
    for c in range(n):
        y_sample[c // 4, (c % 4) * q:(c % 4 + 1) * q] = np.asarray(res.results[c]["y1"], np.float32)[(c % 4) * q:(c % 4 + 1) * q]
    return (y_prompt, y_sample)
```

```python
import math
from contextlib import ExitStack

import numpy as np
import ml_dtypes
import concourse.bass as bass
import concourse.mybir as mybir
from concourse.bass_utils import run_bass_kernel_spmd

F32 = mybir.dt.float32
BF16 = mybir.dt.bfloat16
AF = mybir.ActivationFunctionType
ALU = mybir.AluOpType
AX = mybir.AxisListType

D = 1024
FF = 2816
EPS = 1e-6
NSLOT = 8
SAME_SYNC = True
ACOLS = 52800
PH = 4864
NBF = ml_dtypes.bfloat16
SLOPES = [2.0 ** (-2.0 * (h + 1)) for h in range(4)]


class Prog:
    def __init__(self, nc, stack):
        self.nc = nc
        self.engs = ['pe', 'act', 'dve', 'pool', 'sp']
        self.ops = {e: [] for e in self.engs}
        self.lastw = {}
        self.rd_e = {}
        self.rd_d = {}
        self.ndma = {e: 0 for e in self.engs}
        self.esem = {e: stack.enter_context(nc.semaphore("s_" + e)) for e in self.engs}
        self.dsem = {e: [stack.enter_context(nc.semaphore("d_%s%d" % (e, i))) for i in range(NSLOT)]
                     for e in ('sp', 'pool')}
        self.pending = {e: set() for e in self.engs}

    def add(self, eng, fn, r=(), w=(), dma=False):
        deps = set(self.pending[eng])
        self.pending[eng] = set()
        for b in r:
            t = self.lastw.get(b)
            if t is not None:
                deps.add(t)
        for b in w:
            t = self.lastw.get(b)
            if t is not None:
                deps.add(t)
            for e2, i2 in self.rd_e.get(b, {}).items():
                deps.add(('e', e2, i2))
            for t2 in self.rd_d.get(b, ()):
                deps.add(t2)
        idx = len(self.ops[eng])
        if dma:
            n = self.ndma[eng]
            self.ndma[eng] += 1
            tok = ('d', eng, n)
            if n >= NSLOT:
                deps.add(('d', eng, n - NSLOT))
        else:
            tok = ('e', eng, idx)
        deps = {t for t in deps
                if not (t[0] == 'e' and t[1] == eng and not dma and (eng == 'pe' or not SAME_SYNC))}
        for b in r:
            if dma:
                self.rd_d.setdefault(b, []).append(tok)
            else:
                self.rd_e.setdefault(b, {})[eng] = idx
        for b in w:
            self.lastw[b] = tok
            self.rd_e[b] = {}
            self.rd_d[b] = []
        self.ops[eng].append([fn, deps, tok, False])
        return tok

    def barrier(self):
        toks = set()
        for e in self.engs:
            for op in reversed(self.ops[e]):
                if op[2][0] == 'e':
                    toks.add(op[2])
                    break
            n = self.ndma[e]
            for k in range(max(0, n - NSLOT), n):
                toks.add(('d', e, k))
        for e in self.engs:
            self.pending[e] |= toks
        self.lastw = {}
        self.rd_e = {}
        self.rd_d = {}

    def emit(self, block):
        for e in self.engs:
            for op in self.ops[e]:
                for t in op[1]:
                    if t[0] == 'e':
                        self.ops[t[1]][t[2]][3] = True
        val = {}
        last = {e: 0 for e in self.engs}
        for e in self.engs:
            c = 0
            for i, op in enumerate(self.ops[e]):
                if op[3]:
                    c += 1
                    val[(e, i)] = c
            last[e] = c

        def semval(t):
            if t[0] == 'e':
                return ('e', t[1]), self.esem[t[1]], val[(t[1], t[2])]
            return (('d', t[1], t[2] % NSLOT), self.dsem[t[1]][t[2] % NSLOT],
                    16 * (t[2] // NSLOT + 1))

        def run(e, eo):
            waited = {}
            for fn, deps, tok, ms in self.ops[e]:
                need = {}
                for t in deps:
                    k, s, v = semval(t)
                    if waited.get(k, 0) < v and need.get(k, (None, 0))[1] < v:
                        need[k] = (s, v)
                for k, (s, v) in need.items():
                    eo.wait_ge(s, v)
                    waited[k] = v
                ins = fn(eo)
                if tok[0] == 'd':
                    ins.then_inc(self.dsem[e][tok[2] % NSLOT], 16)
                elif ms:
                    ins.then_inc(self.esem[e], 1)
            if e == 'sp':
                for q in ('sp', 'pool'):
                    n = self.ndma[q]
                    for k in range(max(0, n - NSLOT), n):
                        kk, s, v = semval(('d', q, k))
                        if waited.get(kk, 0) < v:
                            eo.wait_ge(s, v)
                            waited[kk] = v
                for q in self.engs:
                    if q != 'sp' and last[q] > 0:
                        eo.wait_ge(self.esem[q], last[q])

        @block.tensor
        def _(eo):
            run('pe', eo)

        @block.scalar
        def _(eo):
            run('act', eo)

        @block.vector
        def _(eo):
            run('dve', eo)

        @block.gpsimd
        def _(eo):
            run('pool', eo)

        @block.sync
        def _(eo):
            run('sp', eo)


def build(seq_lens, dbg=False, att_full=False, nlayers=2):
    nc = bass.Bass("TRN2", target_bir_lowering=False)
    Smax = max(seq_lens)

    def din(name, shape, dt=F32):
        return nc.dram_tensor(name, list(shape), dt, kind="ExternalInput").ap()

    def dscr(name, shape, dt):
        return nc.dram_tensor(name, list(shape), dt,
                              kind=("ExternalOutput" if dbg else "Internal")).ap()

    xs = [din("x%d" % i, [S, D]) for i, S in enumerate(seq_lens)]
    ys = [nc.dram_tensor("y%d" % i, [S, D], F32, kind="ExternalOutput").ap()
          for i, S in enumerate(seq_lens)]
    w_in = din("w_in", [2, D, 3600])
    w_out = din("w_out", [2, D, D])
    w_f1 = din("w_ffn_in", [2, D, 2 * FF])
    w_f2 = din("w_ffn_out", [2, FF, D])
    d_nw1 = din("norm1_w", [128, 16])
    d_nw2 = din("norm2_w", [128, 16])
    d_fnw = din("final_norm_w", [1, D])
    d_bg = din("b_gate", [1, 32])
    d_cw = din("conv_w", [128, 80])
    d_cb = din("conv_b", [128, 16])
    d_lam = din("lam", [1, 512])
    d_anw = din("att_norm_w", [1, 256])
    d_mnw = din("mlstm_norm_w", [1, 1024])
    d_cf = din("c_f32", [128, 1536])
    d_cbf = din("c_bf16", [128, 640], BF16)
    d_posq = din("posq", [2, 4, Smax], BF16)
    d_kpos = din("kpos", [4, 4, Smax], BF16)

    class _Scr:
        pass
    SCS = []
    for si_, S_ in enumerate(seq_lens):
        o_ = _Scr()
        sfx = "_%d" % si_
        o_.qT = dscr("s_qT" + sfx, [4, 128, S_], BF16)
        o_.kT = dscr("s_kT" + sfx, [4, 128, S_], BF16)
        o_.Vx = dscr("s_Vx" + sfx, [S_, 4, 129], BF16)
        o_.MVx = dscr("s_MVx" + sfx, [S_, 4, 129], BF16)
        o_.MO = dscr("s_MO" + sfx, [S_, 512], F32)
        o_.G = dscr("s_G" + sfx, [S_, 16], F32)
        o_.PRE = dscr("s_pre" + sfx, [8, 128, S_ + 4], F32)
        o_.mqT = dscr("s_mqT" + sfx, [512, S_], BF16)
        o_.mkT = dscr("s_mkT" + sfx, [512, S_], BF16)
        o_.HF = dscr("s_HF" + sfx, [S_, 512], F32)
        o_.mixT = dscr("s_mixT" + sfx, [1024, S_], BF16)
        o_.X1 = dscr("s_X1" + sfx, [S_, D], F32)
        SCS.append(o_)
    cur = [SCS[0]]

    with ExitStack() as st:
        P = Prog(nc, st)
        arena = st.enter_context(nc.sbuf_tensor("arena", [128, ACOLS], F32))
        psq = st.enter_context(nc.psum_tensor("psq", [128, 4, 512], F32))
        ps = [psq[:, i, :] for i in range(4)] + \
             [st.enter_context(nc.psum_tensor("ps%d" % i, [128, 512], F32))[:, :] for i in range(4, 8)]
        block = st.enter_context(nc.Block())

        def V(off, n, dt=F32, pat=None, **kw):
            ap = arena[:, off:off + n]
            if dt is BF16:
                ap = ap.bitcast(BF16)
            if pat:
                ap = ap.rearrange(pat, **kw)
            return ap

        def psb(i):
            return ps[i][:, :].bitcast(BF16)

        def dma(q, out, in_, r=(), w=()):
            P.add(q, lambda e: e.dma_start(out=out, in_=in_), r, w, dma=True)

        def mm(out, lhsT, rhs, start, stop, r=(), w=()):
            P.add('pe', lambda e: e.matmul(out, lhsT=lhsT, rhs=rhs, start=start, stop=stop), r, w)

        def tr(out, in_, ident, r=(), w=()):
            P.add('pe', lambda e: e.transpose(out, in_, ident), r, w)

        def act(out, in_, func, r=(), w=(), **kw):
            P.add('act', lambda e: e.activation(out=out, in_=in_, func=func, **kw), r, w)

        def cp(eng, out, in_, r=(), w=()):
            if eng == 'act':
                P.add('act', lambda e: e.copy(out=out, in_=in_), r, w)
            else:
                P.add(eng, lambda e: e.tensor_copy(out=out, in_=in_), r, w)

        def ts(eng, out, in0, s1, s2, op0, op1=None, r=(), w=()):
            if op1 is None:
                P.add(eng, lambda e: e.tensor_scalar(out=out, in0=in0, scalar1=s1, scalar2=None, op0=op0), r, w)
            else:
                P.add(eng, lambda e: e.tensor_scalar(out=out, in0=in0, scalar1=s1, scalar2=s2, op0=op0, op1=op1), r, w)

        def tt(eng, out, in0, in1, op, r=(), w=()):
            P.add(eng, lambda e: e.tensor_tensor(out=out, in0=in0, in1=in1, op=op), r, w)

        def stt(eng, out, in0, scalar, in1, op0, op1, r=(), w=()):
            P.add(eng, lambda e: e.scalar_tensor_tensor(out=out, in0=in0, scalar=scalar, in1=in1, op0=op0, op1=op1), r, w)

        def rsqrt_chain(v, n, key):
            act(v, v, AF.Sqrt, r=[key], w=[key])
            P.add('dve', lambda e: e.reciprocal(out=v, in_=v), [key], [key])

        o = 0
        cf = V(o, 1536); o += 1536
        cbf = V(o, 320, BF16); o += 320
        nw1 = V(o, 16); o += 16
        nw2 = V(o, 16); o += 16
        cwt = V(o, 80); o += 80
        cbt = V(o, 16); o += 16
        fnwb = V(o, 1024); o += 1024
        bgb = V(o, 32); o += 32
        lamb = V(o, 512); o += 512
        anwb = V(o, 256); o += 256
        mnwb = V(o, 1024); o += 1024
        nlam = V(o, 8); o += 8
        zer = V(o, 8); o += 8
        assert o <= PH
        ident_f = cf[:, 0:128]
        ones_f = cf[:, 128:256]
        tri = [cf[:, 256:384], cf[:, 384:512]]
        msk = [cf[:, 512:1024], cf[:, 1024:1536]]
        ident_b = cbf[:, 0:128]
        dcorr = [cbf[:, 128 + 128 * h:256 + 128 * h] for h in range(4)]

        dma('sp', cf, d_cf[:, :])
        dma('sp', cbf, d_cbf[:, :])
        dma('sp', nw1, d_nw1[:, :])
        dma('sp', nw2, d_nw2[:, :])
        dma('sp', cwt, d_cw[:, :])
        dma('sp', cbt, d_cb[:, :])
        dma('sp', fnwb, d_fnw.partition_broadcast(128))
        dma('sp', bgb, d_bg.partition_broadcast(128))
        dma('sp', lamb, d_lam.partition_broadcast(128))
        dma('sp', anwb, d_anw.partition_broadcast(128))
        dma('sp', mnwb, d_mnw.partition_broadcast(128))
        P.add('dve', lambda e: e.memset(zer, 0.0), [], ['zer'])
        P.barrier()
        lam_init = [0.8 - 0.6 * math.exp(-0.3 * l) for l in range(2)]
        for l in range(2):
            lv = lamb[:, l * 256:(l + 1) * 256].rearrange("p (a b d) -> p a b d", a=2, b=2)
            pr = V(PH, 128, F32, "p (a d) -> p a d", a=2)
            sm2 = V(PH + 128, 2)
            tt('dve', pr, lv[:, :, 0, :], lv[:, :, 1, :], ALU.mult, r=[], w=['pr'])
            P.add('dve', lambda e, sm2=sm2, pr=pr: e.tensor_reduce(out=sm2, in_=pr, axis=AX.X, op=ALU.add), ['pr'], ['sm2'])
            act(sm2, sm2, AF.Exp, r=['sm2'], w=['sm2'])
            stt('dve', nlam[:, l:l + 1], sm2[:, 1:2], -lam_init[l], sm2[:, 0:1], ALU.add, ALU.subtract,
                r=['sm2'], w=['nlam'])
            ts('dve', anwb[:, l * 128:(l + 1) * 128], anwb[:, l * 128:(l + 1) * 128], 1.0 - lam_init[l], None,
               ALU.mult, r=[], w=['anwb'])
        P.barrier()

        def phase1(l, jobs):
            b = PH
            Wb = V(b, 14400, BF16, "p (c n) -> p c n", c=8); b += 14400
            stg = [V(b + i * 1800, 1800, F32, "p (c n) -> p c n", c=8) for i in range(2)]; b += 3600
            xt = [V(b + i * 4096, 4096, F32, "p (s d) -> p s d", s=4) for i in range(2)]; b += 8192
            xn = [V(b + i * 512, 512, BF16) for i in range(2)]; b += 1024
            hT = [V(b + i * 2048, 2048, BF16, "p (c t) -> p c t", c=8) for i in range(2)]; b += 4096
            junk = V(b, 512, BF16); b += 512
            smv = V(b, 16); b += 16
            fo = [V(b + i * 256, 256, BF16) for i in range(4)]; b += 1024
            pret = [V(b + i * 512, 512) for i in range(3)]; b += 1536
            vt = [V(b + i * 258, 258, BF16, "p (h e) -> p h e", h=4) for i in range(2)]; b += 516
            mvt = [V(b + i * 258, 258, BF16, "p (h e) -> p h e", h=4) for i in range(2)]; b += 516
            mot = [V(b + i * 512, 512) for i in range(2)]; b += 1024
            gtt = [V(b + i * 16, 16) for i in range(2)]; b += 32
            assert b <= ACOLS
            wv = w_in[l].rearrange("(c p) n -> p c n", p=128)
            for sl in range(16):
                sk = 'stg%d' % (sl % 2)
                dma('sp', stg[sl % 2], wv[:, :, sl * 225:(sl + 1) * 225], w=[sk])
                for ch in range(8):
                    ts('dve' if ch % 2 == 0 else 'pool', Wb[:, ch, sl * 225:(sl + 1) * 225], stg[sl % 2][:, ch, :],
                       nw1[:, l * 8 + ch:l * 8 + ch + 1], None, ALU.mult, r=[sk], w=[('Wb', sl, ch)])
            for i in range(2):
                P.add('dve', lambda e, a=vt[i][:, :, 128:129]: e.memset(a, 1.0), [], [('vt1', i)])
                P.add('dve', lambda e, a=mvt[i][:, :, 128:129]: e.memset(a, 1.0), [], [('mvt1', i)])
            P.barrier()
            for S, xin, scr_ in jobs:
                cur[0] = scr_
                phase1_job(l, S, xin, Wb, xt, xn, hT, junk, smv, fo, pret, vt, mvt, mot, gtt)

        def phase1_job(l, S, xin, Wb, xt, xn, hT, junk, smv, fo, pret, vt, mvt, mot, gtt):
            NST = S // 512
            for blk in range(8):
                dma('pool', cur[0].PRE[blk, :, 0:2], zer[:, 0:2])
                dma('pool', cur[0].PRE[blk, :, S + 2:S + 4], zer[:, 0:2])
            P.barrier()

            cnt = {'bank': 0, 'ev': 0, 'fo': 0, 'pre': 0}

            def nbank():
                k = 2 + cnt['bank'] % 6
                cnt['bank'] += 1
                return k

            def evac_eng():
                cnt['ev'] += 1
                return 'act' if cnt['ev'] % 2 == 0 else 'dve'

            def normA(s_):
                xb = s_ % 2
                xk = 'xt%d' % xb
                dma('sp', xt[xb], xin[s_ * 512:(s_ + 1) * 512, :].rearrange("(s p) d -> p s d", p=128), w=[xk])
                for sub in range(4):
                    c = (s_ % 2) * 4 + sub
                    act(junk, xt[xb][:, sub, :], AF.Square, r=[xk], w=['junk', ('ss', c)], accum_out=smv[:, c:c + 1])
                    ts('dve', smv[:, c:c + 1], smv[:, c:c + 1], 1.0 / D, EPS, ALU.mult, ALU.add, r=[('ss', c)], w=[('ss', c)])
                    rsqrt_chain(smv[:, c:c + 1], 1, ('ss', c))

            def normB(s_):
                xb = s_ % 2
                xk = 'xt%d' % xb
                for sub in range(4):
                    c = (s_ % 2) * 4 + sub
                    nb = sub % 2
                    act(xn[nb], xt[xb][:, sub, :], AF.Copy, r=[xk, ('ss', c)], w=[('xn', nb)], scale=smv[:, c:c + 1])
                    pb = psb(sub % 2)
                    for ch in range(8):
                        tr(pb[:, ch * 128:(ch + 1) * 128], xn[nb][:, ch * 128:(ch + 1) * 128], ident_b,
                           r=[('xn', nb)], w=[('ps', sub % 2)])
                    cp('dve', hT[xb][:, :, sub * 128:(sub + 1) * 128],
                       pb.rearrange("p (c t) -> p c t", c=8), r=[('ps', sub % 2)], w=[('hT', xb, sub)])

            def mains(s_):
                xb = s_ % 2
                t0 = s_ * 512
                hk = [('hT', xb, sub) for sub in range(4)]
                for cb_ in range(16):
                    if cb_ < 4:
                        col0 = cb_ * 128
                    elif cb_ < 8:
                        col0 = 512 + (cb_ - 4) * 128
                    else:
                        col0 = 1536 + (cb_ - 8) * 128
                    bk = nbank()
                    for ch in range(8):
                        mm(ps[bk][:, :], Wb[:, ch, col0:col0 + 128], hT[xb][:, ch, :], ch == 0, ch == 7,
                           r=hk, w=[('ps', bk)])
                    if cb_ < 8:
                        fi = cnt['fo'] % 4
                        cnt['fo'] += 1
                        eng = evac_eng()
                        if cb_ < 4:
                            cp(eng, fo[fi], ps[bk][:, :], r=[('ps', bk)], w=[('fo', fi)])
                            dma('pool', cur[0].qT[cb_, :, t0:t0 + 512], fo[fi], r=[('fo', fi)])
                        else:
                            if eng == 'act':
                                act(fo[fi], ps[bk][:, :], AF.Copy, r=[('ps', bk)], w=[('fo', fi)], scale=0.125)
                            else:
                                ts('dve', fo[fi], ps[bk][:, :], 0.125, None, ALU.mult, r=[('ps', bk)], w=[('fo', fi)])
                            dma('pool', cur[0].kT[cb_ - 4, :, t0:t0 + 512], fo[fi], r=[('fo', fi)])
                    else:
                        pi = cnt['pre'] % 3
                        cnt['pre'] += 1
                        cp(evac_eng(), pret[pi], ps[bk][:, :], r=[('ps', bk)], w=[('pre', pi)])
                        dma('pool', cur[0].PRE[cb_ - 8, :, 2 + t0:2 + t0 + 512], pret[pi], r=[('pre', pi)])
                for sub in range(4):
                    tk = t0 + sub * 128
                    i2 = sub % 2
                    for which, col0 in (('av', 1024), ('mv', 2560), ('mo', 3072)):
                        bk = nbank()
                        for ch in range(8):
                            mm(ps[bk][:, :], hT[xb][:, ch, sub * 128:(sub + 1) * 128], Wb[:, ch, col0:col0 + 512],
                               ch == 0, ch == 7, r=[hk[sub]], w=[('ps', bk)])
                        src = ps[bk][:, :]
                        if which == 'av':
                            cp(evac_eng(), vt[i2][:, :, 0:128], src.rearrange("p (h e) -> p h e", h=4),
                               r=[('ps', bk), ('vt1', i2)], w=[('vt', i2)])
                            dma('pool', cur[0].Vx[tk:tk + 128, :, :], vt[i2], r=[('vt', i2)])
                        elif which == 'mv':
                            cp(evac_eng(), mvt[i2][:, :, 0:128], src.rearrange("p (h e) -> p h e", h=4),
                               r=[('ps', bk), ('mvt1', i2)], w=[('mvt', i2)])
                            dma('pool', cur[0].MVx[tk:tk + 128, :, :], mvt[i2], r=[('mvt', i2)])
                        else:
                            cp(evac_eng(), mot[i2], src, r=[('ps', bk)], w=[('mot', i2)])
                            dma('pool', cur[0].MO[tk:tk + 128, :], mot[i2], r=[('mot', i2)])
                    bk = nbank()
                    for ch in range(8):
                        mm(ps[bk][:, 0:16], hT[xb][:, ch, sub * 128:(sub + 1) * 128], Wb[:, ch, 3584:3600],
                           ch == 0, ch == 7, r=[hk[sub]], w=[('ps', bk)])
                    tt('dve', gtt[i2], ps[bk][:, 0:16], bgb[:, l * 16:(l + 1) * 16], ALU.add,
                       r=[('ps', bk)], w=[('gtt', i2)])
                    dma('pool', cur[0].G[tk:tk + 128, :], gtt[i2], r=[('gtt', i2)])

            normA(0)
            normB(0)
            for s_ in range(NST):
                if s_ + 1 < NST:
                    normA(s_ + 1)
                mains(s_)
                if s_ + 1 < NST:
                    normB(s_ + 1)
            P.barrier()

        def phase_conv(l, S):
            b = PH
            dg = V(b, 2560, BF16, "p (a k c) -> p a k c", a=8, k=5); b += 2560
            ptile = [V(b + i * 520, 516) for i in range(3)]; b += 1560
            xb = [V(b + i * 260, 258, BF16) for i in range(3)]; b += 780
            so = [V(b + i * 512, 512) for i in range(2)]; b += 1024
            ob = [V(b + i * 256, 256, BF16) for i in range(3)]; b += 768
            for blk in range(8):
                for k in range(5):
                    c0 = (l * 8 + blk) * 5 + k
                    ts('dve' if (blk * 5 + k) % 2 == 0 else 'pool', dg[:, blk, k, :], ident_f, cwt[:, c0:c0 + 1], None,
                       ALU.mult, r=[], w=[('dg', blk, k)])
            P.barrier()
            i = 0
            for s_ in range(S // 512):
                t0 = s_ * 512
                for blk in range(8):
                    pi, ai, bk = i % 3, i % 2, i % 8
                    i += 1
                    dma('sp', ptile[pi], cur[0].PRE[blk, :, t0:t0 + 516], w=[('pt', pi)])
                    cp('dve', xb[pi][:, 0:516], ptile[pi], r=[('pt', pi)], w=[('xb', pi)])
                    for k in range(5):
                        mm(ps[bk][:, :], dg[:, blk, k, :], xb[pi][:, k:k + 512], k == 0, k == 4,
                           r=[('xb', pi)], w=[('ps', bk)])
                    bias = cbt[:, l * 8 + blk:l * 8 + blk + 1]
                    if blk < 4:
                        act(ob[pi], ps[bk][:, :], AF.Silu, r=[('ps', bk)], w=[('ob', pi)], bias=bias)
                        dma('pool', cur[0].mqT[blk * 128:(blk + 1) * 128, t0:t0 + 512], ob[pi], r=[('ob', pi)])
                    else:
                        act(so[ai], ps[bk][:, :], AF.Silu, r=[('ps', bk)], w=[('so', ai)], bias=bias)
                        ts('dve', ob[pi], so[ai], 128.0 ** -0.5, None, ALU.mult,
                           r=[('so', ai)], w=[('ob', pi)])
                        dma('pool', cur[0].mkT[(blk - 4) * 128:(blk - 3) * 128, t0:t0 + 512], ob[pi], r=[('ob', pi)])
            P.barrier()

        def key_tiles(h, qt, NKT):
            if att_full:
                return list(range(NKT))
            dmin = 40.0 / SLOPES[h]
            lo = max(0, int(math.floor((qt * 512 - dmin) / 128.0)))
            hi = min(NKT - 1, int(math.floor((qt * 512 + 511 + dmin) / 128.0)))
            return list(range(lo, hi + 1))

        def phase_att(l, S):
            NKT = S // 128
            b = PH
            kTs = V(b, S, BF16, "p (m t) -> p m t", m=2); b += S
            Vs = V(b, (NKT * 129 + 1) // 2, BF16)[:, 0:NKT * 129].rearrange("p (k e) -> p k e", k=NKT); b += (NKT * 129 + 1) // 2
            qx = [V(b + i * 1024, 1024, BF16, "p (s m t) -> p s m t", s=2, m=2) for i in range(2)]; b += 2048
            pT2 = [V(b + i * 512, 512, BF16, "p (m t) -> p m t", m=2) for i in range(3)]
            pT = [[pT2[i][:, m, :] for m in range(2)] for i in range(3)]; b += 1536
            ov = V(b, 1032, F32, "p (a e) -> p a e", a=8); b += 1032
            av = [V(b + i * 128, 128) for i in range(2)]; b += 256
            jk = V(b, 128); b += 128
            obf = [V(b + i * 64, 64, BF16) for i in range(2)]; b += 128
            ot = [V(b + i * 256, 256, BF16) for i in range(2)]; b += 512
            sm = V(b, 32); b += 32
            assert b <= ACOLS
            accreg = [(4 + (a // 3), (a % 3) * 129) for a in range(8)]
            qi = 0
            pend = [None]
            for h in range(4):
                kst = min(2048, S)
                kkeys, vkeys = [], []
                for c0 in range(0, S, kst):
                    kkeys.append(('kT', c0))
                    dma('sp', kTs[0:64, :, c0:c0 + kst],
                        cur[0].kT[h, :, c0:c0 + kst].rearrange("(m d) t -> d m t", m=2), w=[kkeys[-1]])
                for m in range(2):
                    kkeys.append(('kTp', m))
                    dma('sp', kTs[64:68, m, :], d_kpos[h, :, 0:S], w=[kkeys[-1]])
                vst = min(16, NKT)
                for k0 in range(0, NKT, vst):
                    vkeys.append(('Vs', k0))
                    dma('sp', Vs[:, k0:k0 + vst, :],
                        cur[0].Vx[k0 * 128:(k0 + vst) * 128, h, :].rearrange("(k p) e -> p k e", p=128), w=[vkeys[-1]])
                for qt in range(S // 512):
                    t0 = qt * 512
                    qb = qi % 2
                    qi += 1
                    qk = ('qx', qb)
                    for sg in range(2):
                        dma('sp', qx[qb][0:64, sg, :, :],
                            cur[0].qT[h, :, t0:t0 + 512].rearrange("(m d) t -> d m t", m=2), w=[qk])
                    for m in range(2):
                        dma('sp', qx[qb][64:68, :, m, :],
                            d_posq[:, :, t0:t0 + 512].rearrange("s r t -> r s t"), w=[qk])
                    kts = key_tiles(h, qt, NKT)
                    n = len(kts)

                    def stageA(i):
                        kt = kts[i]
                        for m in range(2):
                            bk = (i % 2) * 2 + m
                            kl = kTs[0:68, m, kt * 128:(kt + 1) * 128]
                            rr = kkeys + [qk]
                            ww = [('ps', bk)]
                            if kt < 4 * qt:
                                mm(ps[bk][:, :], kl, qx[qb][0:68, 0, m, :], True, True, r=rr, w=ww)
                            elif kt > 4 * qt + 3:
                                mm(ps[bk][:, :], kl, qx[qb][0:68, 1, m, :], True, True, r=rr, w=ww)
                            else:
                                d = kt - 4 * qt
                                if d > 0:
                                    mm(ps[bk][:, 0:128 * d], kl, qx[qb][0:68, 1, m, 0:128 * d], True, True, r=rr, w=ww)
                                mm(ps[bk][:, 128 * d:128 * (d + 1)], kl, qx[qb][0:68, 0, m, 128 * d:128 * (d + 1)],
                                   True, False, r=rr, w=ww)
                                mm(ps[bk][:, 128 * d:128 * (d + 1)], ident_b, dcorr[h], False, True, r=rr, w=ww)
                                if d < 3:
                                    mm(ps[bk][:, 128 * (d + 1):512], kl, qx[qb][0:68, 0, m, 128 * (d + 1):512],
                                       True, True, r=rr, w=ww)

                    def stageB(i):
                        b0 = (i % 2) * 2
                        act(pT2[i % 3], psq[:, b0:b0 + 2, :], AF.Exp, r=[('ps', b0), ('ps', b0 + 1)],
                            w=[('pT', i % 3, 0), ('pT', i % 3, 1)])

                    def stageC(i):
                        kt = kts[i]
                        for m in range(2):
                            for sub in range(4):
                                bk, c0 = accreg[m * 4 + sub]
                                a_ = m * 4 + sub
                                mm(ps[bk][:, c0:c0 + 129], pT[i % 3][m][:, sub * 128:(sub + 1) * 128], Vs[:, kt, :],
                                   i == 0 and a_ % 3 == 0, i == n - 1 and (a_ % 3 == 2 or a_ == 7),
                                   r=[('pT', i % 3, m)] + vkeys, w=[('acc', bk)])

                    stageA(0)
                    if n > 1:
                        stageA(1)
                    for i in range(n):
                        stageB(i)
                        stageC(i)
                        if i + 2 < n:
                            stageA(i + 2)
                        if i == 1 and pend[0] is not None:
                            pend[0]()
                            pend[0] = None
                    for bk in range(4, 7):
                        na = 3 if bk < 6 else 2
                        cp('dve', ov[:, (bk - 4) * 3:(bk - 4) * 3 + na, :],
                           ps[bk][:, 0:na * 129].rearrange("p (a e) -> p a e", a=na),
                           r=[('acc', bk)], w=[('ov', bk)])
                    ovk = [('ov', 4), ('ov', 5), ('ov', 6)]
                    rinv = sm[:, 0:8]
                    P.add('dve', lambda e, rinv=rinv, ov=ov: e.reciprocal(out=rinv.unsqueeze(2), in_=ov[:, :, 128:129]),
                          ovk, ['rinv'])
                    ts('dve', sm[:, 8:12], sm[:, 4:8], nlam[:, l:l + 1], None, ALU.mult, r=['rinv'], w=['rl'])
                    oi = qi % 2

                    def part2(h=h, t0=t0, oi=oi):
                        pb = psb(7)
                        for sub in range(4):
                            tr(pb[:, sub * 128:(sub + 1) * 128], obf_all[oi][:, sub * 128:(sub + 1) * 128], ident_b,
                               r=[('obf', oi)], w=[('ps', 7)])
                        cp('act', ot[oi], pb[:, 0:512], r=[('ps', 7)], w=[('ot', oi)])
                        dma('pool', cur[0].mixT[h * 128:(h + 1) * 128, t0:t0 + 512], ot[oi], r=[('ot', oi)])

                    for sub in range(4):
                        ai = sub % 2
                        ts('dve', av[ai], ov[:, sub, 0:128], sm[:, sub:sub + 1], None, ALU.mult,
                           r=ovk + ['rinv'], w=[('av', ai)])
                        stt('dve', av[ai], ov[:, 4 + sub, 0:128], sm[:, 8 + sub:9 + sub], av[ai], ALU.mult, ALU.add,
                            r=ovk + ['rl', ('av', ai)], w=[('av', ai)])
                        sk = ('ssq', sub)
                        act(jk, av[ai], AF.Square, r=[('av', ai)], w=['jk', sk], accum_out=sm[:, 16 + sub:17 + sub])
                        ts('dve', sm[:, 16 + sub:17 + sub], sm[:, 16 + sub:17 + sub], 1.0 / 128, EPS, ALU.mult, ALU.add,
                           r=[sk], w=[sk])
                        rsqrt_chain(sm[:, 16 + sub:17 + sub], 1, sk)
                        stt('dve', obf_all[oi][:, sub * 128:(sub + 1) * 128], av[ai], sm[:, 16 + sub:17 + sub],
                            anwb[:, l * 128:(l + 1) * 128], ALU.mult, ALU.mult, r=[('av', ai), sk], w=[('obf', oi)])
                    pend[0] = part2
            if pend[0] is not None:
                pend[0]()
                pend[0] = None
            P.barrier()

        obf_all = [V(ACOLS - 512 + i * 256, 256, BF16) for i in range(2)]

        def phase_mlstm(l, S):
            NC = S // 128
            b = PH

            def A(n, dt=F32, pat=None, **kw):
                nonlocal b
                v = V(b, n, dt, pat, **kw)
                b += n
                return v

            def A2(n, dt=F32, pat=None, **kw):
                return [A(n, dt, pat, **kw) for _ in range(2)]
            qc = A2(256, BF16, "p (h t) -> p h t", h=4)
            kc = A2(256, BF16, "p (h t) -> p h t", h=4)
            vx = A2(258, BF16, "p (h e) -> p h e", h=4)
            gt = A2(16)
            mo = A2(512)
            hfl = A2(512)
            g4 = A2(32)
            pe16 = A2(16)
            ex = A2(16)
            vs1 = A2(258, BF16, "p (h e) -> p h e", h=4)
            vs2 = A2(258, BF16, "p (h e) -> p h e", h=4)
            kk = A2(256, BF16, "p (h d) -> p h d", h=4)
            smk = A2(256, BF16, "p (h j) -> p h j", h=4)
            nd = A(516, F32, "p (h e) -> p h e", h=4)
            dd = A(16)
            hd = A2(512, F32, "p (h e) -> p h e", h=4)
            Cst = A(516, F32, "p (h e) -> p h e", h=4)
            Cb = A(258, BF16, "p (h e) -> p h e", h=4)
            hs = A(512, F32, "p (h e) -> p h e", h=4)
            sq = A(512, F32, "p (h e) -> p h e", h=4)
            og = A(512, F32, "p (h e) -> p h e", h=4)
            memb = A(256, BF16, "p (h e) -> p h e", h=4)
            mst = A2(1024, BF16, "p (h t) -> p h t", h=4)
            assert b <= ACOLS - 512
            for d in range(2):
                P.add('dve', lambda e: e.memset(Cst, 0.0), [], [('Cst', h) for h in range(4)])
                P.add('pool', lambda e: e.memset(Cb, 0.0), [], ['Cb'])
                order = list(range(NC)) if d == 0 else list(range(NC - 1, -1, -1))

                def stage1(ii, d=d, order=order):
                    c = order[ii]
                    t0 = c * 128
                    bi = ii % 2
                    dma('sp', qc[bi], cur[0].mqT[:, t0:t0 + 128].rearrange("(h d) t -> d h t", h=4), w=[('qc', bi)])
                    dma('sp', kc[bi], cur[0].mkT[:, t0:t0 + 128].rearrange("(h d) t -> d h t", h=4), w=[('kc', bi)])
                    dma('sp', vx[bi], cur[0].MVx[t0:t0 + 128, :, :], w=[('vx', bi)])
                    dma('sp', gt[bi], cur[0].G[t0:t0 + 128, :], w=[('gt', bi)])
                    if d == 1:
                        dma('sp', mo[bi], cur[0].MO[t0:t0 + 128, :], w=[('mo', bi)])
                        dma('sp', hfl[bi], cur[0].HF[t0:t0 + 128, :], w=[('hfl', bi)])
                    fg = gt[bi][:, 8 + 4 * d:12 + 4 * d]
                    ig = gt[bi][:, 4 * d:4 * d + 4]
                    g_ = g4[bi]
                    ab, t1, l1, lf = g_[:, 0:4], g_[:, 4:8], g_[:, 8:12], g_[:, 12:16]
                    stt('dve', ab, fg, -1.0, fg, ALU.mult, ALU.max, r=[('gt', bi)], w=[('ab', bi)])
                    act(t1, ab, AF.Exp, r=[('ab', bi)], w=[('t1', bi)], scale=-1.0)
                    act(l1, t1, AF.Ln, r=[('t1', bi)], w=[('l1', bi)], bias=1.0)
                    stt('dve', lf, fg, 0.0, l1, ALU.min, ALU.subtract, r=[('gt', bi), ('l1', bi)], w=[('lf', bi)])
                    mm(ps[0][:, 0:4], tri[d], lf, True, True, r=[('lf', bi)], w=[('ps', 0)])
                    mm(ps[0][:, 4:8], ones_f, lf, True, True, r=[('lf', bi)], w=[('ps', 0)])
                    p16 = pe16[bi]
                    pk = ('pe16', bi)
                    pks = [('pe16', bi, j) for j in range(4)]
                    tt('dve', p16[:, 0:4], ig, ps[0][:, 0:4], ALU.subtract, r=[('gt', bi), ('ps', 0)], w=[pks[0]])
                    cp('dve', p16[:, 4:8], ps[0][:, 0:4], r=[('ps', 0)], w=[pks[1]])
                    tt('dve', p16[:, 8:12], p16[:, 0:4], ps[0][:, 4:8], ALU.add, r=[pks[0], ('ps', 0)], w=[pks[2]])
                    cp('dve', p16[:, 12:16], ps[0][:, 4:8], r=[('ps', 0)], w=[pks[3]])
                    ek = ('ex', bi)
                    act(ex[bi], p16, AF.Exp, r=pks, w=[ek])
                    tt('dve', vs1[bi], vx[bi], ex[bi][:, 0:4].unsqueeze(2).to_broadcast([128, 4, 129]), ALU.mult,
                       r=[('vx', bi), ek], w=[('vs1', bi)])
                    tt('pool', vs2[bi], vx[bi], ex[bi][:, 8:12].unsqueeze(2).to_broadcast([128, 4, 129]), ALU.mult,
                       r=[('vx', bi), ek], w=[('vs2', bi)])
                    pkk = psb(1)
                    for h in range(4):
                        tr(pkk[:, h * 128:(h + 1) * 128], kc[bi][:, h, :], ident_b, r=[('kc', bi)], w=[('ps', 1)])
                    cp('act', kk[bi], pkk[:, 0:512].rearrange("p (h d) -> p h d", h=4), r=[('ps', 1)], w=[('kk', bi)])
                    for h in range(4):
                        mm(ps[2][:, h * 128:(h + 1) * 128], kc[bi][:, h, :], qc[bi][:, h, :], True, True,
                           r=[('kc', bi), ('qc', bi)], w=[('ps', 2)])
                    tt('dve', smk[bi], ps[2][:, :].rearrange("p (h j) -> p h j", h=4),
                       msk[d].rearrange("p (h j) -> p h j", h=4), ALU.mult, r=[('ps', 2)], w=[('smk', bi)])

                def stage2(ii, d=d, order=order):
                    c = order[ii]
                    t0 = c * 128
                    bi = ii % 2
                    ek = ('ex', bi)
                    for h in range(4):
                        bk = 5 + h // 2
                        c0 = (h % 2) * 129
                        mm(ps[bk][:, c0:c0 + 129], kk[bi][:, h, :], vs2[bi][:, h, :], True, True,
                           r=[('kk', bi), ('vs2', bi)], w=[('ps', bk)])
                    for h in range(4):
                        bk = 3 + h // 2
                        c0 = (h % 2) * 129
                        mm(ps[bk][:, c0:c0 + 129], qc[bi][:, h, :], Cb[:, h, :], True, False,
                           r=[('qc', bi), 'Cb'], w=[('ps', bk)])
                        mm(ps[bk][:, c0:c0 + 129], smk[bi][:, h, :], vs1[bi][:, h, :], False, True,
                           r=[('smk', bi), ('vs1', bi)], w=[('ps', bk)])
                    for h in range(4):
                        bk = 5 + h // 2
                        c0 = (h % 2) * 129
                        stt('dve', Cst[:, h, :], Cst[:, h, :], ex[bi][:, 12 + h:13 + h], ps[bk][:, c0:c0 + 129],
                            ALU.mult, ALU.add, r=[('Cst', h), ek, ('ps', bk)], w=[('Cst', h)])
                    cp('act', Cb, Cst, r=[('Cst', h) for h in range(4)], w=['Cb'])
                    for half in range(2):
                        tt('dve', nd[:, 2 * half:2 * half + 2, :],
                           ps[3 + half][:, 0:258].rearrange("p (h e) -> p h e", h=2),
                           ex[bi][:, 4 + 2 * half:6 + 2 * half].unsqueeze(2).to_broadcast([128, 2, 129]), ALU.mult,
                           r=[('ps', 3 + half), ek], w=[('nd', half)])
                    ndk = [('nd', 0), ('nd', 1)]
                    stt('dve', dd[:, 0:4].unsqueeze(2), nd[:, :, 128:129], -1.0, nd[:, :, 128:129], ALU.mult, ALU.max,
                        r=ndk, w=['dd'])
                    ts('dve', dd[:, 0:4], dd[:, 0:4], 1.0, None, ALU.max, r=['dd'], w=['dd'])
                    P.add('dve', lambda e, dd=dd: e.reciprocal(out=dd[:, 4:8], in_=dd[:, 0:4]), ['dd'], ['rd'])
                    hi = ii % 2
                    tt('dve', hd[hi], nd[:, :, 0:128], dd[:, 4:8].unsqueeze(2).to_broadcast([128, 4, 128]), ALU.mult,
                       r=ndk + ['rd'], w=[('hd', hi)])
                    if d == 0:
                        dma('sp', cur[0].HF[t0:t0 + 128, :], hd[hi].rearrange("p h e -> p (h e)"), r=[('hd', hi)])
                    else:
                        tt('dve', hs, hd[hi], hfl[bi].rearrange("p (h e) -> p h e", h=4), ALU.add,
                           r=[('hd', hi), ('hfl', bi)], w=['hs'])
                        tt('pool', sq, hs, hs, ALU.mult, r=['hs'], w=['sq'])
                        P.add('dve', lambda e, dd=dd, sq=sq: e.tensor_reduce(out=dd[:, 8:12], in_=sq, axis=AX.X, op=ALU.add),
                              ['sq'], ['ss4'])
                        ts('dve', dd[:, 8:12], dd[:, 8:12], 1.0 / 128, EPS, ALU.mult, ALU.add, r=['ss4'], w=['ss4'])
                        rsqrt_chain(dd[:, 8:12], 4, 'ss4')
                        tt('dve', hs, hs, dd[:, 8:12].unsqueeze(2).to_broadcast([128, 4, 128]), ALU.mult,
                           r=['hs', 'ss4'], w=['hs'])
                        tt('pool', hs, hs, mnwb[:, l * 512:(l + 1) * 512].rearrange("p (h e) -> p h e", h=4), ALU.mult,
                           r=['hs'], w=['hs'])
                        act(og, mo[bi].rearrange("p (h e) -> p h e", h=4), AF.Sigmoid, r=[('mo', bi)], w=['og'])
                        tt('dve', memb, og, hs, ALU.mult, r=['og', 'hs'], w=['memb'])
                        pm = psb(7)
                        for h in range(4):
                            tr(pm[:, h * 128:(h + 1) * 128], memb[:, h, :], ident_b, r=['memb'], w=[('ps', 7)])
                        grp = c // 4
                        gi = grp % 2
                        cp('act', mst[gi][:, :, (c % 4) * 128:(c % 4 + 1) * 128],
                           pm[:, 0:512].rearrange("p (h t) -> p h t", h=4), r=[('ps', 7)], w=[('mst', gi)])
                        if c % 4 == 0:
                            dma('sp', cur[0].mixT[512:1024, grp * 512:(grp + 1) * 512].rearrange("(h e) t -> e h t", h=4),
                                mst[gi], r=[('mst', gi)])

                stage1(0)
                for ii in range(NC):
                    if ii + 1 < NC:
                        stage1(ii + 1)
                    stage2(ii)
                P.barrier()

        def phase3(l, jobs):
            b = PH
            Wo = V(b, 4096, BF16, "p (c n) -> p c n", c=8); b += 4096
            W1 = V(b, 22528, BF16, "p (c n) -> p c n", c=8); b += 22528
            W2 = V(b, 11264, BF16, "p (c n) -> p c n", c=22); b += 11264
            xt = V(b, 2048, F32, "p (s d) -> p s d", s=2); b += 2048
            sb0 = b
            mx = V(b, 1024, BF16, "p (c t) -> p c t", c=8); b += 1024
            hT = V(b, 1024, BF16, "p (c t) -> p c t", c=8); b += 1024
            aT = V(b, 2816, BF16, "p (c t) -> p c t", c=22); b += 2816
            xn = V(b, 512, BF16); b += 512
            junk = V(b, 512, BF16); b += 512
            sg = [V(b + i * 256, 256) for i in range(2)]; b += 512
            smv = V(b, 16); b += 16
            assert b <= ACOLS - 512, b
            stg = [V(sb0 + i * 2048, 2048, F32, "p (c n) -> p c n", c=8) for i in range(2)]
            stg2 = [V(sb0 + i * 2048, 2048, F32, "p (c n) -> p c n", c=2) for i in range(2)]
            si = 0
            wv = w_out[l].rearrange("(c p) n -> p c n", p=128)
            for sl in range(4):
                sk = 'stg%d' % (si % 2)
                dma('sp', stg[si % 2], wv[:, :, sl * 256:(sl + 1) * 256], w=[sk])
                for ch in range(8):
                    cp('dve' if ch % 2 == 0 else 'pool', Wo[:, ch, sl * 256:(sl + 1) * 256], stg[si % 2][:, ch, :],
                       r=[sk], w=[('Wo', sl, ch)])
                si += 1
            wv = w_f1[l].rearrange("(c p) n -> p c n", p=128)
            for sl in range(22):
                sk = 'stg%d' % (si % 2)
                dma('sp', stg[si % 2], wv[:, :, sl * 256:(sl + 1) * 256], w=[sk])
                for ch in range(8):
                    ts('dve' if ch % 2 == 0 else 'pool', W1[:, ch, sl * 256:(sl + 1) * 256], stg[si % 2][:, ch, :],
                       nw2[:, l * 8 + ch:l * 8 + ch + 1], None, ALU.mult, r=[sk], w=[('W1', sl, ch)])
                si += 1
            wv = w_f2[l].rearrange("(c p) n -> p c n", p=128)
            for sl in range(11):
                sk = 'stg%d' % (si % 2)
                dma('sp', stg2[si % 2], wv[:, 2 * sl:2 * sl + 2, :], w=[sk])
                for ch in range(2):
                    cp('dve' if ch % 2 == 0 else 'pool', W2[:, 2 * sl + ch, :], stg2[si % 2][:, ch, :],
                       r=[sk], w=[('W2', sl, ch)])
                si += 1
            P.barrier()
            for S, xin, xout, final, scr_ in jobs:
                cur[0] = scr_
                phase3_job(l, S, xin, xout, final, Wo, W1, W2, xt, mx, hT, aT, xn, junk, sg, smv)

        def phase3_job(l, S, xin, xout, final, Wo, W1, W2, xt, mx, hT, aT, xn, junk, sg, smv):
            NT = S // 256
            cnt = {'bank': 0}

            def nbank():
                k = 1 + cnt['bank'] % 7
                cnt['bank'] += 1
                return k

            for it in range(NT):
                t0 = it * 256
                dma('sp', xt, xin[t0:t0 + 256, :].rearrange("(s p) d -> p s d", p=128), w=['xt'])
                dma('sp', mx, cur[0].mixT[:, t0:t0 + 256].rearrange("(c p) t -> p c t", p=128), w=['mx'])
                for sub in range(2):
                    for n in range(2):
                        bk = nbank()
                        for ch in range(8):
                            mm(ps[bk][:, :], mx[:, ch, sub * 128:(sub + 1) * 128], Wo[:, ch, n * 512:(n + 1) * 512],
                               ch == 0, ch == 7, r=['mx'], w=[('ps', bk)])
                        tt('dve', xt[:, sub, n * 512:(n + 1) * 512], xt[:, sub, n * 512:(n + 1) * 512], ps[bk][:, :],
                           ALU.add, r=[('ps', bk), 'xt'], w=['xt'])
                for sub in range(2):
                    act(junk, xt[:, sub, :], AF.Square, r=['xt'], w=['junk', ('ss', sub)], accum_out=smv[:, sub:sub + 1])
                    ts('dve', smv[:, sub:sub + 1], smv[:, sub:sub + 1], 1.0 / D, EPS, ALU.mult, ALU.add,
                       r=[('ss', sub)], w=[('ss', sub)])
                    rsqrt_chain(smv[:, sub:sub + 1], 1, ('ss', sub))
                    act(xn, xt[:, sub, :], AF.Copy, r=['xt', ('ss', sub)], w=['xn'], scale=smv[:, sub:sub + 1])
                    pb = psb(0)
                    for ch in range(8):
                        tr(pb[:, ch * 128:(ch + 1) * 128], xn[:, ch * 128:(ch + 1) * 128], ident_b, r=['xn'], w=[('ps', 0)])
                    cp('dve', hT[:, :, sub * 128:(sub + 1) * 128], pb.rearrange("p (c t) -> p c t", c=8),
                       r=[('ps', 0)], w=[('hT', sub)])
                hk = [('hT', 0), ('hT', 1)]
                for j in range(22):
                    ba = nbank()
                    for ch in range(8):
                        mm(ps[ba][:, 0:256], W1[:, ch, j * 128:(j + 1) * 128], hT[:, ch, :], ch == 0, ch == 7,
                           r=hk, w=[('ps', ba)])
                    bb = nbank()
                    for ch in range(8):
                        mm(ps[bb][:, 0:256], W1[:, ch, FF + j * 128:FF + (j + 1) * 128], hT[:, ch, :], ch == 0, ch == 7,
                           r=hk, w=[('ps', bb)])
                    act(sg[j % 2], ps[ba][:, 0:256], AF.Silu, r=[('ps', ba)], w=[('sg', j % 2)])
                    tt('dve', aT[:, j, :], sg[j % 2], ps[bb][:, 0:256], ALU.mult, r=[('sg', j % 2), ('ps', bb)],
                       w=[('aT', j)])
                ak = [('aT', j) for j in range(22)]
                for sub in range(2):
                    for n in range(2):
                        bk = nbank()
                        for j in range(22):
                            mm(ps[bk][:, :], aT[:, j, sub * 128:(sub + 1) * 128], W2[:, j, n * 512:(n + 1) * 512],
                               j == 0, j == 21, r=ak, w=[('ps', bk)])
                        tt('dve', xt[:, sub, n * 512:(n + 1) * 512], xt[:, sub, n * 512:(n + 1) * 512], ps[bk][:, :],
                           ALU.add, r=[('ps', bk), 'xt'], w=['xt'])
                if final:
                    for sub in range(2):
                        c = 4 + sub
                        act(junk, xt[:, sub, :], AF.Square, r=['xt'], w=['junk', ('ss', c)], accum_out=smv[:, c:c + 1])
                        ts('dve', smv[:, c:c + 1], smv[:, c:c + 1], 1.0 / D, EPS, ALU.mult, ALU.add,
                           r=[('ss', c)], w=[('ss', c)])
                        rsqrt_chain(smv[:, c:c + 1], 1, ('ss', c))
                        stt('dve', xt[:, sub, :], xt[:, sub, :], smv[:, c:c + 1], fnwb, ALU.mult, ALU.mult,
                            r=['xt', ('ss', c)], w=['xt'])
                dma('pool', xout[t0:t0 + 256, :].rearrange("(s p) d -> p s d", p=128), xt, r=['xt'])
            P.barrier()

        for l in range(nlayers):
            xin_ = [xs[i] if l == 0 else SCS[i].X1[0:S, :] for i, S in enumerate(seq_lens)]
            xout_ = [ys[i] if l == 1 else SCS[i].X1[0:S, :] for i, S in enumerate(seq_lens)]
            phase1(l, [(S, xin_[i], SCS[i]) for i, S in enumerate(seq_lens)])
            for i, S in enumerate(seq_lens):
                cur[0] = SCS[i]
                phase_conv(l, S)
                phase_att(l, S)
                phase_mlstm(l, S)
            phase3(l, [(S, xin_[i], xout_[i], l == 1, SCS[i]) for i, S in enumerate(seq_lens)])
        P.emit(block)
    return nc


def host_consts(Smax):
    cf = np.zeros((128, 1536), np.float32)
    a = np.arange(128)
    cf[:, 0:128] = np.eye(128)
    cf[:, 128:256] = 1.0
    le = (a[:, None] <= a[None, :]).astype(np.float32)
    ge = (a[:, None] >= a[None, :]).astype(np.float32)
    cf[:, 256:384] = le
    cf[:, 384:512] = ge
    cf[:, 512:1024] = np.tile(le, (1, 4))
    cf[:, 1024:1536] = np.tile(ge, (1, 4))
    cb = np.zeros((128, 640), np.float32)
    cb[:, 0:128] = np.eye(128)
    for h in range(4):
        cb[:, 128 + 128 * h:256 + 128 * h] = -2.0 * SLOPES[h] * np.maximum(a[:, None] - a[None, :], 0)
    t = np.arange(Smax)
    ta, tb = (t // 128).astype(np.float32), (t % 128).astype(np.float32)
    posq = np.zeros((2, 4, Smax), np.float32)
    for s, sg in enumerate((1.0, -1.0)):
        posq[s, 0] = sg * ta
        posq[s, 1] = sg * tb
        posq[s, 2] = sg
        posq[s, 3] = sg
    kpos = np.zeros((4, 4, Smax), np.float32)
    for h in range(4):
        kpos[h, 0] = -SLOPES[h] * 128
        kpos[h, 1] = -SLOPES[h]
        kpos[h, 2] = SLOPES[h] * 128 * ta
        kpos[h, 3] = SLOPES[h] * tb
    return {"c_f32": cf, "c_bf16": cb.astype(NBF), "posq": posq.astype(NBF), "kpos": kpos.astype(NBF)}


def host_weights(norm1_w, w_in, b_gate, conv_w, conv_b, lam, att_norm_w, mlstm_norm_w, w_out, norm2_w,
                 w_ffn_in, w_ffn_out, final_norm_w):
    f = lambda a: np.ascontiguousarray(np.asarray(a, np.float32))
    m = {}
    m["w_in"] = f(w_in)
    m["w_out"] = f(w_out)
    m["w_ffn_in"] = f(w_ffn_in)
    m["w_ffn_out"] = f(w_ffn_out)
    m["norm1_w"] = f(np.asarray(norm1_w).reshape(2, 8, 128).transpose(2, 0, 1).reshape(128, 16))
    m["norm2_w"] = f(np.asarray(norm2_w).reshape(2, 8, 128).transpose(2, 0, 1).reshape(128, 16))
    m["final_norm_w"] = f(np.asarray(final_norm_w).reshape(1, D))
    m["b_gate"] = f(np.asarray(b_gate).reshape(1, 32))
    m["conv_w"] = f(np.asarray(conv_w).reshape(2, 5, 8, 128).transpose(3, 0, 2, 1).reshape(128, 80))
    m["conv_b"] = f(np.asarray(conv_b).reshape(2, 8, 128).transpose(2, 0, 1).reshape(128, 16))
    m["lam"] = f(np.asarray(lam).reshape(1, 512))
    m["att_norm_w"] = f(np.asarray(att_norm_w).reshape(1, 256))
    m["mlstm_norm_w"] = f(np.asarray(mlstm_norm_w).reshape(1, 1024))
    return m


_CACHE = {}


def kernel(x_prompt, x_sample, norm1_w, w_in, b_gate, conv_w, conv_b, lam, att_norm_w, mlstm_norm_w,
           w_out, norm2_w, w_ffn_in, w_ffn_out, final_norm_w):
    x_prompt = np.asarray(x_prompt, np.float32)
    x_sample = np.asarray(x_sample, np.float32)
    B, S, _ = x_prompt.shape
    DB, DS, _ = x_sample.shape
    n = 8
    key = (S, DS)
    if key not in _CACHE:
        _CACHE[key] = build([S, DS])
    nc = _CACHE[key]
    base = host_weights(norm1_w, w_in, b_gate, conv_w, conv_b, lam, att_norm_w, mlstm_norm_w, w_out, norm2_w,
                        w_ffn_in, w_ffn_out, final_norm_w)
    base.update(host_consts(max(S, DS)))
    in_maps = []
    for c in range(n):
        m = dict(base)
        m["x0"] = np.ascontiguousarray(x_prompt[c])
        m["x1"] = np.ascontiguousarray(x_sample[c // 4])
        in_maps.append(m)
    res = run_bass_kernel_spmd(nc, in_maps, core_ids=list(range(n)))
    y_prompt = np.stack([np.asarray(res.results[c]["y0"], np.float32) for c in range(n)], axis=0)
    q = DS // 4
    y_sample = np.zeros((DB, DS, D), np.float32)
    for c in range(n):
        y_sample[c // 4, (c % 4) * q:(c % 4 + 1) * q] = np.asarray(res.results[c]["y1"], np.float32)[(c % 4) * q:(c % 4 + 1) * q]
    return (y_prompt, y_sample)
```
